# Optimizing a Trainium2 kernel written in Bass

```python
import math
import jax, jax.numpy as jnp
from jax import lax
import numpy as np

D_MODEL = 1024
BATCH = 8
SEQ = 2048
DEPTH = 4
DEC_BATCH = 8
DEC_SEQ = 16
PAST_LEN = 4096

CHUNK = 64
Q_BLOCK = 128
HD_A = 64
H_A = D_MODEL // (2 * HD_A)
DH_B = 64
H_B = D_MODEL // (4 * DH_B)
W_A = H_A * HD_A
W_B = H_B * 2 * DH_B
D_FF = -(-8 * D_MODEL // (3 * 256)) * 256
PLE_DIM = 256
IN_SPLITS = (W_A, W_A, W_A, H_A, W_B, W_B, W_B, D_MODEL, D_MODEL)
IN_COLS = 3 * W_A + H_A + 3 * W_B + 2 * D_MODEL
ALPHA = (2 * DEPTH) ** 0.25
BETA = (8 * DEPTH) ** -0.25
LN_EPS = 1e-5
RMS_EPS = 1e-5
NEG_INF = -1e30

kernel_name = 'fox_diff_hybrid_stream_encoder'


def _split_cols(a, sizes):
    offs = []
    acc = 0
    for s in sizes[:-1]:
        acc += s
        offs.append(acc)
    return jnp.split(a, offs, axis=-1)


def _layer_norm(x, g, b):
    xf = x.astype(jnp.float32)
    mu = jnp.mean(xf, axis=-1, keepdims=True)
    xc = xf - mu
    var = jnp.mean(xc * xc, axis=-1, keepdims=True)
    y = xc * lax.rsqrt(var + LN_EPS) * g.astype(jnp.float32) + b.astype(jnp.float32)
    return y.astype(x.dtype)


def _sweep_queries(block_fn, q_arrays, q_pos):
    t_q = q_pos.shape[0]
    if t_q <= Q_BLOCK:
        return block_fn(q_arrays, q_pos)
    n_blk = t_q // Q_BLOCK
    blocked = tuple(jnp.moveaxis(a.reshape(a.shape[:1] + (n_blk, Q_BLOCK) + a.shape[2:]), 1, 0)
                    for a in q_arrays)
    out = lax.map(lambda xs: block_fn(xs[0], xs[1]), (blocked, q_pos.reshape(n_blk, Q_BLOCK)))
    out = jnp.moveaxis(out, 0, 1)
    return out.reshape(out.shape[:1] + (t_q,) + out.shape[3:])


def _fox_attention(q, k, v, c_q, c_k, q_pos, k_pos):
    scale = HD_A ** -0.5
    c_k_t = jnp.swapaxes(c_k, 1, 2)[:, :, None, :]

    def block(qa, qp):
        qb, cq = qa
        s = jnp.einsum('bqhd,bkhd->bhqk', qb, k).astype(jnp.float32) * scale
        s = s + jnp.swapaxes(cq, 1, 2)[..., None] - c_k_t
        mask = k_pos[None, :] <= qp[:, None]
        s = jnp.where(mask, s, NEG_INF)
        p = jax.nn.softmax(s, axis=-1)
        return jnp.einsum('bhqk,bkhd->bqhd', p.astype(v.dtype), v)

    return _sweep_queries(block, (q, c_q), q_pos)


def _diff_attention(q, k, v, lam, q_pos, k_pos):
    scale = DH_B ** -0.5
    slopes = 2.0 ** (-8.0 * jnp.arange(1, H_B + 1, dtype=jnp.float32) / H_B)
    k_chunk = k_pos // CHUNK

    def block(qa, qp):
        (qb,) = qa
        s = jnp.einsum('bqhcd,bkhcd->bchqk', qb, k).astype(jnp.float32) * scale
        dist = jnp.abs(qp[:, None] - k_pos[None, :]).astype(jnp.float32)
        s = s - slopes[:, None, None] * dist[None]
        mask = k_chunk[None, :] <= (qp // CHUNK)[:, None]
        s = jnp.where(mask, s, NEG_INF)
        p = jax.nn.softmax(s, axis=-1)
        a = p[:, 0] - lam * p[:, 1]
        return jnp.einsum('bhqk,bkhe->bqhe', a.astype(v.dtype), v)

    return _sweep_queries(block, (q,), q_pos)


def _hybrid_layer(x, p, past, q_pos, k_pos, lambda_init,
                  w_in, b_f, lq1, lk1, lq2, lk2, g_diff, w_ba, w_bb, w_o, ln1_g, ln1_b,
                  w_g, w_u, w_d, w_pg, w_pp, ln2_g, ln2_b):
    bsz, t, _ = x.shape
    proj = x @ w_in
    qa, ka, va, fa, qb, kb, vb, ga, gb = _split_cols(proj, IN_SPLITS)
    qa = qa.reshape(bsz, t, H_A, HD_A)
    ka = ka.reshape(bsz, t, H_A, HD_A)
    va = va.reshape(bsz, t, H_A, HD_A)
    logf = jax.nn.log_sigmoid((fa + b_f).astype(jnp.float32))
    qb = qb.reshape(bsz, t, H_B, 2, DH_B)
    kb = kb.reshape(bsz, t, H_B, 2, DH_B)
    vb = vb.reshape(bsz, t, H_B, 2 * DH_B)

    if past is None:
        ka_all, va_all, logf_all, kb_all, vb_all = ka, va, logf, kb, vb
    else:
        pk, pv, plf, pkb, pvb = past
        ka_all = jnp.concatenate([pk, ka], axis=1)
        va_all = jnp.concatenate([pv, va], axis=1)
        logf_all = jnp.concatenate([plf.astype(jnp.float32), logf], axis=1)
        kb_all = jnp.concatenate([pkb, kb], axis=1)
        vb_all = jnp.concatenate([pvb, vb], axis=1)

    c_all = jnp.cumsum(logf_all, axis=1)
    c_q = c_all[:, -t:]
    oa = _fox_attention(qa, ka_all, va_all, c_q, c_all, q_pos, k_pos)

    f32 = jnp.float32
    lam = (jnp.exp(jnp.sum(lq1.astype(f32) * lk1.astype(f32)))
           - jnp.exp(jnp.sum(lq2.astype(f32) * lk2.astype(f32))) + lambda_init)
    ob = _diff_attention(qb, kb_all, vb_all, lam, q_pos, k_pos).astype(f32)
    ob = ob * lax.rsqrt(jnp.mean(ob * ob, axis=-1, keepdims=True) + RMS_EPS)
    ob = (ob * g_diff.astype(f32).reshape(H_B, 2 * DH_B) * (1.0 - lambda_init)).astype(x.dtype)

    merged = (jax.nn.sigmoid(ga) * (oa.reshape(bsz, t, W_A) @ w_ba)
              + jax.nn.sigmoid(gb) * (ob.reshape(bsz, t, W_B) @ w_bb))
    x = _layer_norm(ALPHA * x + merged @ w_o, ln1_g, ln1_b)

    ffn = (jax.nn.silu(x @ w_g) * (x @ w_u)) @ w_d
    ple = jax.nn.sigmoid(x @ w_pg) * (p @ w_pp)
    x = _layer_norm(ALPHA * x + ffn + ple, ln2_g, ln2_b)
    return x, (ka, va, logf, kb, vb)


def setup_inputs(seed: int = 0) -> dict:
    key = jax.random.key(seed)
    ks = jax.random.split(key, 32)
    f32 = jnp.float32
    nrm = lambda k, shape, s: jax.random.normal(k, shape, f32) * s
    return {
        'x_prompt': nrm(ks[0], (BATCH, SEQ, D_MODEL), 1.0),
        'x_sample': nrm(ks[1], (DEC_BATCH, DEC_SEQ, D_MODEL), 1.0),
        'p_prompt': nrm(ks[2], (DEPTH, BATCH, SEQ, PLE_DIM), 1.0),
        'p_sample': nrm(ks[3], (DEPTH, DEC_BATCH, DEC_SEQ, PLE_DIM), 1.0),
        'cache_fox_k': nrm(ks[4], (DEPTH, DEC_BATCH, PAST_LEN, H_A, HD_A), 1.0),
        'cache_fox_v': nrm(ks[5], (DEPTH, DEC_BATCH, PAST_LEN, H_A, HD_A), 1.0),
        'cache_fox_logf': jax.nn.log_sigmoid(
            jax.random.uniform(ks[6], (DEPTH, DEC_BATCH, PAST_LEN, H_A), f32, 1.0, 4.0)),
        'cache_diff_k': nrm(ks[7], (DEPTH, DEC_BATCH, PAST_LEN, H_B, 2, DH_B), 1.0),
        'cache_diff_v': nrm(ks[8], (DEPTH, DEC_BATCH, PAST_LEN, H_B, 2 * DH_B), 1.0),
        'w_in': nrm(ks[9], (DEPTH, D_MODEL, IN_COLS), D_MODEL ** -0.5),
        'b_forget': jax.random.uniform(ks[10], (DEPTH, H_A), f32, 1.0, 4.0),
        'lambda_q1': nrm(ks[11], (DEPTH, DH_B), 0.1),
        'lambda_k1': nrm(ks[12], (DEPTH, DH_B), 0.1),
        'lambda_q2': nrm(ks[13], (DEPTH, DH_B), 0.1),
        'lambda_k2': nrm(ks[14], (DEPTH, DH_B), 0.1),
        'diff_norm_g': 1.0 + nrm(ks[15], (DEPTH, W_B), 0.02),
        'w_branch_fox': nrm(ks[16], (DEPTH, W_A, D_MODEL), BETA * W_A ** -0.5),
        'w_branch_diff': nrm(ks[17], (DEPTH, W_B, D_MODEL), BETA * W_B ** -0.5),
        'w_out': nrm(ks[18], (DEPTH, D_MODEL, D_MODEL), BETA * D_MODEL ** -0.5),
        'ln1_g': 1.0 + nrm(ks[19], (DEPTH, D_MODEL), 0.02),
        'ln1_b': nrm(ks[20], (DEPTH, D_MODEL), 0.02),
        'w_ffn_gate': nrm(ks[21], (DEPTH, D_MODEL, D_FF), D_MODEL ** -0.5),
        'w_ffn_up': nrm(ks[22], (DEPTH, D_MODEL, D_FF), D_MODEL ** -0.5),
        'w_ffn_down': nrm(ks[23], (DEPTH, D_FF, D_MODEL), BETA * D_FF ** -0.5),
        'w_ple_gate': nrm(ks[24], (DEPTH, D_MODEL, D_MODEL), D_MODEL ** -0.5),
        'w_ple_proj': nrm(ks[25], (DEPTH, PLE_DIM, D_MODEL), BETA * PLE_DIM ** -0.5),
        'ln2_g': 1.0 + nrm(ks[26], (DEPTH, D_MODEL), 0.02),
        'ln2_b': nrm(ks[27], (DEPTH, D_MODEL), 0.02),
    }


def reference(x_prompt, x_sample, p_prompt, p_sample, cache_fox_k, cache_fox_v, cache_fox_logf,
              cache_diff_k, cache_diff_v, w_in, b_forget, lambda_q1, lambda_k1, lambda_q2, lambda_k2,
              diff_norm_g, w_branch_fox, w_branch_diff, w_out, ln1_g, ln1_b, w_ffn_gate, w_ffn_up,
              w_ffn_down, w_ple_gate, w_ple_proj, ln2_g, ln2_b):
    t_p = x_prompt.shape[1]
    t_s = x_sample.shape[1]
    past_len = cache_fox_k.shape[2]
    pos_p = jnp.arange(t_p, dtype=jnp.int32)
    q_pos_s = past_len + jnp.arange(t_s, dtype=jnp.int32)
    k_pos_s = jnp.arange(past_len + t_s, dtype=jnp.int32)

    hp, hs = x_prompt, x_sample
    new_p = ([], [], [], [], [])
    new_s = ([], [], [], [], [])
    for l in range(DEPTH):
        lambda_init = 0.8 - 0.6 * math.exp(-0.3 * l)
        lw = (w_in[l], b_forget[l], lambda_q1[l], lambda_k1[l], lambda_q2[l], lambda_k2[l],
              diff_norm_g[l], w_branch_fox[l], w_branch_diff[l], w_out[l], ln1_g[l], ln1_b[l],
              w_ffn_gate[l], w_ffn_up[l], w_ffn_down[l], w_ple_gate[l], w_ple_proj[l],
              ln2_g[l], ln2_b[l])
        hp, rows_p = _hybrid_layer(hp, p_prompt[l], None, pos_p, pos_p, lambda_init, *lw)
        past = (cache_fox_k[l], cache_fox_v[l], cache_fox_logf[l], cache_diff_k[l], cache_diff_v[l])
        hs, rows_s = _hybrid_layer(hs, p_sample[l], past, q_pos_s, k_pos_s, lambda_init, *lw)
        for lst, r in zip(new_p, rows_p):
            lst.append(r)
        for lst, r in zip(new_s, rows_s):
            lst.append(r)

    fox_k_p, fox_v_p, fox_lf_p, diff_k_p, diff_v_p = [jnp.stack(a) for a in new_p]
    fox_k_s, fox_v_s, fox_lf_s, diff_k_s, diff_v_s = [jnp.stack(a) for a in new_s]
    return (hp, hs, fox_k_p, fox_v_p, fox_lf_p, diff_k_p, diff_v_p,
            fox_k_s, fox_v_s, fox_lf_s, diff_k_s, diff_v_s)
```

```python
import itertools
import math
from contextlib import ExitStack

import numpy as np
import ml_dtypes
import concourse.bass as bass
import concourse.mybir as mybir
from concourse.bass_utils import run_bass_kernel_spmd

F32 = mybir.dt.float32
BF16 = mybir.dt.bfloat16
AF = mybir.ActivationFunctionType
ALU = mybir.AluOpType
AX = mybir.AxisListType

D = 1024
NCH = 8
DFF = 2816
PLE = 256
TS = 16
INC = 5128
ALPHA = 8.0 ** 0.25
LN_EPS = 1e-5
RMS_EPS = 1e-5
CELL = 32
SEG = 4000
NPOOL = 16


def lambda_init(l):
    return 0.8 - 0.6 * math.exp(-0.3 * l)


ENGS = ("pe", "act", "dve", "pool", "sp")
EIDX = {e: i for i, e in enumerate(ENGS)}


class Buf:
    def __init__(self, nc, name, nbytes):
        nbytes = (nbytes + 63) // 64 * 64
        self.name = name
        self.nbytes = nbytes
        self.t = nc.alloc_sbuf_tensor(name, [128, nbytes // 2], BF16)
        nc_ = (nbytes + CELL - 1) // CELL
        self.last_w = np.full(nc_, -1, dtype=np.int64)
        self.last_r = np.full((len(ENGS), nc_), -1, dtype=np.int64)


class Sub:
    __slots__ = ("ap", "buf", "ivs")

    def __init__(self, ap, buf, ivs):
        self.ap, self.buf, self.ivs = ap, buf, ivs


class View:
    def __init__(self, buf, off, shape, dtype):
        self.buf, self.off, self.shape, self.dtype = buf, off, tuple(shape), dtype
        self.es = 4 if dtype == F32 else 2
        n = int(np.prod(shape))
        assert off % 4 == 0 and off + n * self.es <= buf.nbytes, (buf.name, off, shape, buf.nbytes)
        base = buf.t[:, off // 2: off // 2 + n * self.es // 2]
        if dtype == F32:
            base = base.bitcast(F32)
        if len(shape) == 2:
            base = base.rearrange("p (a b) -> p a b", b=shape[1])
        elif len(shape) == 3:
            base = base.rearrange("p (a b c) -> p a b c", b=shape[1], c=shape[2])
        self.ap = base
        self.strides = [int(np.prod(shape[k + 1:])) for k in range(len(shape))]

    def __getitem__(self, idx):
        if not isinstance(idx, tuple):
            idx = (idx,)
        fidx = list(idx[1:]) + [slice(None)] * (len(self.shape) - len(idx) + 1)
        ap = self.ap[(idx[0], *fidx)]
        rngs = []
        for d, ix in zip(self.shape, fidx):
            if isinstance(ix, int):
                rngs.append((ix, ix + 1))
            else:
                s, e, _ = ix.indices(d)
                rngs.append((s, e))
        ivs = []
        k = len(rngs) - 1
        while k > 0 and rngs[k] == (0, self.shape[k]):
            k -= 1
        for o in itertools.product(*[range(a, b) for a, b in rngs[:k]]):
            base = sum(i * st for i, st in zip(o, self.strides[:k]))
            lo = self.off + (base + rngs[k][0] * self.strides[k]) * self.es
            hi = self.off + (base + rngs[k][1] * self.strides[k]) * self.es
            ivs.append((lo, hi))
        return Sub(ap, self.buf, ivs)


class Op:
    __slots__ = ("eng", "fn", "deps", "dma", "needs_inc", "sem", "val", "semkey", "clock")

    def __init__(self, eng, fn, deps, dma):
        self.eng, self.fn, self.deps, self.dma = eng, fn, deps, dma
        self.needs_inc = dma
        self.sem = self.val = self.semkey = self.clock = None


class Prog:
    def __init__(self):
        self.ops = []
        self.kw = {}
        self.kr = {}

    def op(self, eng, fn, R=(), W=(), dma=False):
        idx = len(self.ops)
        e = EIDX[eng]
        deps = set()
        for r in R:
            if isinstance(r, Sub):
                for lo, hi in r.ivs:
                    c0, c1 = lo // CELL, (hi + CELL - 1) // CELL
                    deps.update(np.unique(r.buf.last_w[c0:c1]).tolist())
            else:
                w = self.kw.get(r)
                if w is not None:
                    deps.add(w)
        for w_ in W:
            if isinstance(w_, Sub):
                for lo, hi in w_.ivs:
                    c0, c1 = lo // CELL, (hi + CELL - 1) // CELL
                    deps.update(np.unique(w_.buf.last_w[c0:c1]).tolist())
                    deps.update(np.unique(w_.buf.last_r[:, c0:c1]).tolist())
            else:
                w = self.kw.get(w_)
                if w is not None:
                    deps.add(w)
                deps.update(self.kr.get(w_, {}).values())
        for r in R:
            if isinstance(r, Sub):
                for lo, hi in r.ivs:
                    c0, c1 = lo // CELL, (hi + CELL - 1) // CELL
                    r.buf.last_r[e, c0:c1] = idx
            else:
                self.kr.setdefault(r, {})[eng] = idx
        for w_ in W:
            if isinstance(w_, Sub):
                for lo, hi in w_.ivs:
                    c0, c1 = lo // CELL, (hi + CELL - 1) // CELL
                    w_.buf.last_w[c0:c1] = idx
                    w_.buf.last_r[:, c0:c1] = -1
            else:
                self.kw[w_] = idx
                self.kr[w_] = {}
        deps.discard(-1)
        if eng == "pe" and not dma:
            deps = {d for d in deps if not (self.ops[d].eng == "pe" and not self.ops[d].dma)}
        self.ops.append(Op(eng, fn, deps, dma))
        return idx

    def emit(self, nc, stack):
        ops = self.ops
        for o in ops:
            for d in o.deps:
                ops[d].needs_inc = True
        ninc = {e: 0 for e in ENGS}
        for o in ops:
            if o.needs_inc and not o.dma:
                ninc[o.eng] += 1
        esems = {e: [stack.enter_context(nc.semaphore(f"s_{e}_{k}")) for k in range(ninc[e] // SEG + 1)]
                 for e in ENGS}
        dsems = {e: [stack.enter_context(nc.semaphore(f"d_{e}_{k}")) for k in range(NPOOL)]
                 for e in ("sp", "pool")}
        known = {e: {} for e in ENGS}
        cnt = {e: 0 for e in ENGS}
        dcnt = {e: 0 for e in ENGS}
        dhist = {e: [] for e in ENGS}
        streams = {e: [] for e in ENGS}
        nwait = 0
        for o in ops:
            kn = known[o.eng]
            st = streams[o.eng]

            def need(dop):
                nonlocal nwait
                if kn.get(dop.semkey, 0) < dop.val:
                    st.append(("w", dop.sem, dop.val))
                    nwait += 1
                    for k, v in dop.clock.items():
                        if kn.get(k, 0) < v:
                            kn[k] = v
            for d in sorted(o.deps):
                need(ops[d])
            if o.dma:
                k = dcnt[o.eng]
                dcnt[o.eng] += 1
                if k >= NPOOL:
                    need(dhist[o.eng][k - NPOOL])
                o.sem = dsems[o.eng][k % NPOOL]
                o.val = 16 * (k // NPOOL + 1)
                o.semkey = ("d", o.eng, k % NPOOL)
                o.clock = dict(kn)
                o.clock[o.semkey] = o.val
                dhist[o.eng].append(o)
                st.append(("o", o.fn, o.sem, 16))
            elif o.needs_inc:
                g = cnt[o.eng]
                cnt[o.eng] += 1
                o.sem = esems[o.eng][g // SEG]
                o.val = g % SEG + 1
                o.semkey = (o.eng, g // SEG)
                o.clock = dict(kn)
                o.clock[o.semkey] = o.val
                st.append(("o", o.fn, o.sem, 1))
            else:
                st.append(("o", o.fn, None, 0))
        self.stats = dict(nops=len(ops), nwait=nwait, ninc=dict(cnt), ndma=dict(dcnt))
        return streams


def replay(e, stream):
    for it in stream:
        if it[0] == "w":
            e.wait_ge(it[1], it[2])
        else:
            _, fn, sem, inc = it
            if fn is None:
                continue
            ins = fn(e)
            if sem is not None:
                ins.then_inc(sem, inc)


def make_consts(TP, PAST):
    T = TP + TS
    ident = np.eye(128, dtype=np.float32)
    s = np.arange(128)[:, None]
    t = np.arange(128)[None, :]
    dm = np.zeros((128, 5, 128), np.float32)
    dm[:, 0, :] = np.where(s > t, -1e30, 0.0)
    for h in range(4):
        slope = 2.0 ** (-2.0 * (h + 1))
        v = np.where(s > t, -2.0 * slope * (s - t), 0.0)
        v = np.where((s // 64) > (t // 64), -1e30, v)
        dm[:, 1 + h, :] = v
    U = (s <= t).astype(np.float32)
    A = np.zeros((32, 16, 64), np.float32)
    for p in range(4):
        for hh in range(2):
            off = 32 * hh
            h = 2 * p + hh
            aq = A[:, 0 * 8 + 0 * 4 + p, :]
            ak = A[:, 0 * 8 + 1 * 4 + p, :]
            for k in range(3):
                aq[8 * k + h, off + k] = 1.0
                aq[24, off + 3 + k] = 1.0
                ak[24, off + k] = 1.0
                ak[8 * k + h, off + 3 + k] = -1.0
            slope = 2.0 ** (-2.0 * (p + 1))
            aq = A[:, 1 * 8 + 0 * 4 + p, :]
            ak = A[:, 1 * 8 + 1 * 4 + p, :]
            aq[25, off + 0] = -slope
            aq[26, off + 1] = -slope
            aq[24, off + 2] = 1.0
            aq[24, off + 3] = 1.0
            ak[24, off + 0] = 1.0
            ak[24, off + 1] = 1.0
            ak[25, off + 2] = slope
            ak[26, off + 3] = slope
    rq = np.zeros((3, T), np.float32)
    rq[0] = 1.0
    tp = np.arange(TP)
    rq[1, :TP] = 128 * (tp // 128)
    rq[2, :TP] = tp % 128
    rq[1, TP:] = 0.0
    rq[2, TP:] = np.arange(TS) - 15.0
    rk = np.zeros((3, PAST), np.float32)
    rk[0] = 1.0
    sp = np.arange(PAST)
    rk[1] = -128.0 * (PAST // 128 - sp // 128)
    rk[2] = sp % 128 - 15.0
    qb = np.zeros((32, 2, 128), np.float32)
    dms = np.zeros((16, 2, 128), np.float32)
    mask = np.zeros((128, 2, 512), np.float32)
    for h in range(8):
        for k in range(3):
            qb[8 * k + h, 0, 16 * h:16 * h + 16] = -1.0
        dms[:, 0, 16 * h:16 * h + 16] = dm[:16, 0, :16]
        mask[16 * h:16 * h + 16, 0, 64 * h:64 * h + 64] = 1.0
    for c in range(2):
        for h in range(4):
            slope = 2.0 ** (-2.0 * (h + 1))
            cb = 64 * c + 16 * h
            qb[25, 1, cb:cb + 16] = slope
            qb[26, 1, cb:cb + 16] = slope
            dms[:, 1, cb:cb + 16] = dm[:16, 1 + h, :16]
            mask[cb:cb + 16, 1, 128 * h:128 * h + 128] = 1.0
    return dict(c_ident=ident, c_dm=dm.reshape(128, 640), c_U=U, c_A=A.reshape(32, 1024),
                c_rq=rq.astype(ml_dtypes.bfloat16), c_rk=rk.astype(ml_dtypes.bfloat16),
                c_qbias=qb.reshape(32, 256).astype(ml_dtypes.bfloat16), c_dms=dms.reshape(16, 256).astype(ml_dtypes.bfloat16),
                c_mask=mask.reshape(128, 1024).astype(ml_dtypes.bfloat16))


def build(cfg):
    TP, PAST, DEPTH = cfg["TP"], cfg["PAST"], cfg["DEPTH"]
    dbg_names = cfg.get("dbg", ())
    T = TP + TS
    NTP = TP // 128
    NT = NTP + 1
    tiles = [(i * 128, 128) for i in range(NTP)] + [(TP, TS)]
    NGP = TP // 512
    groups = [(g * 512, 512) for g in range(NGP)] + [(TP, TS)]
    NG = NGP + 1
    NPT = PAST // 128
    QT = min(8, NPT)
    NQ = NPT // QT
    TB = T * 2

    nc = bass.Bass("TRN2", target_bir_lowering=False)

    def din(name, shape):
        return nc.dram_tensor(name, list(shape), F32, kind="ExternalInput").ap()

    def dout(name, shape):
        return nc.dram_tensor(name, list(shape), F32, kind="ExternalOutput").ap()

    x0 = din("x0", [T, D])
    p_in = din("p", [DEPTH, T, PLE])
    ckf = din("ckf", [DEPTH, PAST, 512])
    cvf = din("cvf", [DEPTH, PAST, 512])
    clf = din("clf", [DEPTH, PAST, 8])
    ckd = din("ckd", [DEPTH, PAST, 512])
    cvd = din("cvd", [DEPTH, PAST, 512])
    w_in = din("w_in", [DEPTH, D, INC])
    b_f = din("b_forget", [DEPTH, 8])
    lq1 = din("lambda_q1", [1, DEPTH * 64])
    lk1 = din("lambda_k1", [1, DEPTH * 64])
    lq2 = din("lambda_q2", [1, DEPTH * 64])
    lk2 = din("lambda_k2", [1, DEPTH * 64])
    g_diff = din("diff_norm_g", [DEPTH, 512])
    w_ba = din("w_branch_fox", [DEPTH, 512, D])
    w_bb = din("w_branch_diff", [DEPTH, 512, D])
    w_o = din("w_out", [DEPTH, D, D])
    ln1_g = din("ln1_g", [DEPTH, D])
    ln1_b = din("ln1_b", [DEPTH, D])
    w_g = din("w_ffn_gate", [DEPTH, D, DFF])
    w_u = din("w_ffn_up", [DEPTH, D, DFF])
    w_d = din("w_ffn_down", [DEPTH, DFF, D])
    w_pg = din("w_ple_gate", [DEPTH, D, D])
    w_pp = din("w_ple_proj", [DEPTH, PLE, D])
    ln2_g = din("ln2_g", [DEPTH, D])
    ln2_b = din("ln2_b", [DEPTH, D])
    c_ident = din("c_ident", [128, 128])
    c_dm = din("c_dm", [128, 640])
    c_U = din("c_U", [128, 128])
    c_A = din("c_A", [32, 1024])
    c_rq = nc.dram_tensor("c_rq", [3, T], BF16, kind="ExternalInput").ap()
    c_rk = nc.dram_tensor("c_rk", [3, PAST], BF16, kind="ExternalInput").ap()
    c_qbias = nc.dram_tensor("c_qbias", [32, 256], BF16, kind="ExternalInput").ap()
    c_dms = nc.dram_tensor("c_dms", [16, 256], BF16, kind="ExternalInput").ap()
    c_mask = nc.dram_tensor("c_mask", [128, 1024], BF16, kind="ExternalInput").ap()

    y_out = dout("y", [T, D])
    o_fk = dout("o_fk", [DEPTH, T, 512])
    o_fv = dout("o_fv", [DEPTH, T, 512])
    o_lf = dout("o_lf", [DEPTH, T, 8])
    o_dk = dout("o_dk", [DEPTH, T, 512])
    o_dv = dout("o_dv", [DEPTH, T, 512])
    xres = nc.dram_tensor("xres", [T, D], F32).ap()

    P = Prog()
    out_dmas = []

    b_xT = Buf(nc, "xT", NCH * TB)
    xT = View(b_xT, 0, (NCH, T), BF16)
    SLOTB = 8192
    slots = [Buf(nc, f"slot{k}", SLOTB) for k in range(4)]
    b_c = Buf(nc, "consts", 256 + 1280 + 512 + 512 + 256 + 2048 + 256)
    ident = View(b_c, 0, (128,), BF16)
    dm = View(b_c, 256, (5, 128), BF16)
    Utri = View(b_c, 1536, (128,), F32)
    ones32 = View(b_c, 2048, (128,), F32)
    onesdiv = View(b_c, 2560, (128,), BF16)
    Amat = View(b_c, 2816, (16, 64), BF16)
    onesb = View(b_c, 4864, (128,), BF16)
    NLS = (NPT + 1) * 8
    smb = 0

    def salloc(n):
        nonlocal smb
        o = smb
        smb += (n + 31) // 32 * 32
        return o
    offs = {}
    for nm, n in [("lam", DEPTH * 4), ("neglam", DEPTH * 4), ("bfbc", DEPTH * 32), ("gsc", DEPTH * 16),
                  ("ls1", DEPTH * 4), ("ls2", DEPTH * 4),
                  ("Zf", NT * 32), ("LFa", NT * 32), ("LFe", NT * 32), ("LFl", NT * 32), ("LF", NT * 32),
                  ("Ls", NLS * 4), ("Zex", NLS * 4), ("Chi", NLS * 2), ("R1", NLS * 4), ("R2", NLS * 4),
                  ("TSq", NT * 48), ("TSk", NPT * 48), ("st6", 192), ("mv", 32), ("rstd", 16), ("Wf", 128)]:
        offs[nm] = salloc(n)
    b_s = Buf(nc, "small", smb)

    def sv(nm, shape, dt):
        return View(b_s, offs[nm], shape, dt)
    lam = sv("lam", (DEPTH,), F32)
    neglam = sv("neglam", (DEPTH,), F32)
    bfbc = sv("bfbc", (DEPTH * 8,), F32)
    gsc = sv("gsc", (DEPTH * 4,), F32)
    ls1 = sv("ls1", (DEPTH,), F32)
    ls2 = sv("ls2", (DEPTH,), F32)
    Zf = sv("Zf", (NT, 8), F32)
    LFa = sv("LFa", (NT, 8), F32)
    LFe = sv("LFe", (NT, 8), F32)
    LFl = sv("LFl", (NT, 8), F32)
    LF = sv("LF", (NT, 8), F32)
    Ls = sv("Ls", (NPT + 1, 8), F32)
    Zex = sv("Zex", (NPT + 1, 8), F32)
    Chi = sv("Chi", (NPT + 1, 8), BF16)
    R1 = sv("R1", (NPT + 1, 8), F32)
    R2 = sv("R2", (NPT + 1, 8), F32)
    TSq = sv("TSq", (NT, 24), BF16)
    TSk = sv("TSk", (NPT, 24), BF16)
    st6 = sv("st6", (4, 2, 6), F32)
    mv = sv("mv", (4, 2), F32)
    rstd = sv("rstd", (4,), F32)
    Wf = sv("Wf", (8, 8), BF16)

    Z_SZ = NT * 4096
    AB0 = 13 * TB + QT * 256 + 3 * 1024 + 2 * 2048 + 3 * 2048
    S0 = max(Z_SZ, AB0)
    S0 = (S0 + 63) // 64 * 64
    S1 = max(NT * 2048, 8 * TB)
    S1 = (S1 + 63) // 64 * 64
    S2 = 26624
    b_ov = Buf(nc, "ov", S0 + S1 + S2)
    zacc = View(b_ov, 0, (NT, D), F32)
    oaT = View(b_ov, 0, (4, T), BF16)
    obT = View(b_ov, 4 * TB, (4, T), BF16)
    QKt = [View(b_ov, (8 + k) * TB, (T,), BF16) for k in range(4)]
    Rq = View(b_ov, 12 * TB, (T,), BF16)
    o = 13 * TB
    Rk = View(b_ov, o, (QT * 128,), BF16); o += QT * 256
    PTb = [View(b_ov, o + k * 1024, (512,), BF16) for k in range(3)]; o += 3072
    RECb = [View(b_ov, o + k * 2048, (512,), F32) for k in range(2)]; o += 4096
    OCb = [View(b_ov, o + k * 2048, (512,), F32) for k in range(3)]; o += 6144
    assert o <= S0
    Vst = View(b_ov, S0, (NT, 8, 128), BF16)
    mergedT = View(b_ov, S0, (8, T), BF16)
    hT = View(b_ov, S0, (4, T), BF16)
    o2 = S0 + S1
    ystage = [View(b_ov, o2 + k * 2048, (512,), F32) for k in range(4)]
    Krb = [View(b_ov, o2 + 4096 + k * 4096, (4, 512), BF16) for k in range(2)]
    Vrb = [View(b_ov, o2 + 12288 + k * 4096, (4, 512), BF16) for k in range(2)]
    VnewS = View(b_ov, o2 + 20480, (512,), BF16)
    QblkS = View(b_ov, o2 + 21504, (4, 128), BF16)
    KnewS = View(b_ov, o2 + 22528, (4, 16), BF16)
    SQb = View(b_ov, o2 + 25600, (512,), BF16)
    RSb = RECb[0]
    VstD = View(b_ov, S0, (NT, 4, 128), BF16)
    if 4 * TB >= 14912:
        b_ss, SB0 = b_ov, 8 * TB
    else:
        b_ss, SB0 = Buf(nc, "sscr", 14912), 0
    KTb = [View(b_ss, SB0 + k * 2048, (1024,), BF16) for k in range(4)]
    RkTb = [View(b_ss, SB0 + 8192 + k * 1024, (512,), BF16) for k in range(2)]
    PTn = [View(b_ss, SB0 + 10240 + k * 1024, (512,), BF16) for k in range(2)]
    Zm = View(b_ss, SB0 + 4096, (512,), F32)
    MaskS = View(b_ov, o2 + 25600, (512,), BF16)
    QbS = View(b_ss, SB0 + 12288, (128,), BF16)
    DmS = View(b_ss, SB0 + 12544, (128,), BF16)
    Zs = View(b_ss, SB0 + 12800, (128,), F32)
    Znb = View(b_ss, SB0 + 13312, (128,), BF16)
    Znb2 = View(b_ss, SB0 + 13568, (128,), BF16)
    SQs = View(b_ss, SB0 + 13824, (128,), F32)
    Gs = View(b_ss, SB0 + 14336, (128,), F32)
    rsS = View(b_ss, SB0 + 14848, (8,), F32)
    ssS = View(b_ss, SB0 + 14880, (8,), F32)
    assert (b_ss is not b_ov) or SB0 + 14912 <= 12 * TB
    xrow = [View(b_ov, o2 + k * 4096, (D,), F32) for k in range(2)]
    LNG = View(b_ov, o2 + 8192, (D,), F32)
    LNB = View(b_ov, o2 + 12288, (D,), F32)
    XB = View(b_ov, o2 + 16384, (D,), BF16)
    XBs = [XB, View(b_ov, o2 + 24576, (D,), BF16)]
    CS = [View(b_ov, o2 + 18432 + k * 2048, (512,), F32) for k in range(4)]
    ptok = [View(b_ov, o2 + 18432 + k * 512, (PLE,), BF16) for k in range(2)]
    pTt = [View(b_ov, o2 + 19456 + k * 512, (2, 128), BF16) for k in range(2)]
    SGe = [View(b_ov, o2 + 20480 + k * 2048, (512,), F32) for k in range(2)]
    PLe = [View(b_ov, o2 + 4096 + k * 2048, (512,), F32) for k in range(1)]
    SLe = [View(b_ov, o2 + k * 2048, (512,), F32) for k in range(2)]
    lqa = View(b_ov, o2, (DEPTH, 64), F32)
    lqb = View(b_ov, o2 + 4096, (DEPTH, 64), F32)

    PS = [nc.alloc_psum_tensor(f"ps{b}", [128, 512], F32) for b in range(8)]
    PSB = [t.bitcast(BF16) for t in PS]

    def psk(b):
        return ("ps", b)
    rot = {"all": 0, "st": 0, "acc": 0, "pt": 0, "ys": 0}

    def bank():
        allowed = {"all": (0, 1, 2, 3, 4, 5, 6, 7), "proj": (3, 4, 5, 6, 7), "sample": (7, 0, 1, 2)}[rot.get("mode", "all")]
        while True:
            rot["all"] = (rot["all"] + 1) % 8
            if rot["all"] in allowed:
                return rot["all"]
    ST_BANKS = (0, 1, 2)
    ACC_BANKS = (3, 4, 5, 6)

    def stbank():
        rot["st"] = (rot["st"] + 1) % 3
        return ST_BANKS[rot["st"]]

    def ptbuf():
        rot["pt"] = (rot["pt"] + 1) % 3
        return PTb[rot["pt"]]

    def pe_mm(mms, R, W):
        def fn(e, mms=mms):
            ins = None
            for (o_, l_, r_, s_, t_) in mms:
                ins = e.matmul(o_, lhsT=l_, rhs=r_, start=s_, stop=t_)
            return ins
        return P.op("pe", fn, R, W)

    def pe_tr(trs, R, W):
        def fn(e, trs=trs):
            ins = None
            for (o_, i_, id_) in trs:
                ins = e.transpose(o_, i_, id_)
            return ins
        return P.op("pe", fn, R, W)

    def dma(eng, out_ap, in_ap, R, W, is_out=False, nonc=False):
        def fn(e, o_=out_ap, i_=in_ap):
            if nonc:
                return e.dma_start(out=o_, in_=i_, allow_slow_non_contiguous=True)
            return e.dma_start(out=o_, in_=i_)
        k = P.op(eng, fn, R, W, dma=True)
        if is_out:
            out_dmas.append(k)
        return k

    def act(out, in_, func, R, W, **kw):
        return P.op("act", lambda e: e.activation(out=out, in_=in_, func=func, **kw), R, W)

    def copy(eng, out, in_, R, W):
        if eng == "act":
            return act(out, in_, AF.Copy, R, W)
        return P.op(eng, lambda e: e.tensor_copy(out=out, in_=in_), R, W)

    def tt(eng, out, in0, in1, op, R, W):
        return P.op(eng, lambda e: e.tensor_tensor(out=out, in0=in0, in1=in1, op=op), R, W)

    def tsc(eng, out, in0, s1, s2, op0, op1, R, W):
        if op1 is None:
            return P.op(eng, lambda e: e.tensor_scalar(out=out, in0=in0, scalar1=s1, scalar2=None, op0=op0), R, W)
        return P.op(eng, lambda e: e.tensor_scalar(out=out, in0=in0, scalar1=s1, scalar2=s2, op0=op0, op1=op1), R, W)

    def stt(eng, out, in0, scalar, in1, op0, op1, R, W):
        return P.op(eng, lambda e: e.scalar_tensor_tensor(out=out, in0=in0, scalar=scalar, in1=in1, op0=op0, op1=op1), R, W)

    def memset(eng, sub, val):
        return P.op(eng, lambda e: e.memset(sub.ap, val), (), [sub])

    dbg_outs = {}

    def dbg(name, sub, shape):
        if name not in dbg_names:
            return
        d = nc.dram_tensor("dbg_" + name, list(shape), sub.ap.dtype, kind="ExternalOutput").ap()
        dbg_outs[name] = d
        dma("sp", d, sub.ap, [sub], [("dbg", name)], is_out=True)

    dma("pool", ident[:].ap, c_ident[:, :], (), [ident[:]])
    dma("pool", dm[:].ap, c_dm.rearrange("p (a b) -> p a b", b=128), (), [dm[:]])
    dma("sp", Utri[:].ap, c_U[:, :], (), [Utri[:]])
    dma("pool", Amat[0:32].ap, c_A.rearrange("p (a b) -> p a b", b=64), (), [Amat[0:32]])
    memset("dve", ones32[:], 1.0)
    memset("dve", onesdiv[:], 1.0 / 128.0)
    memset("dve", onesb[:], 1.0)
    for v_ in (Zf, LF, Ls, Zex, TSq, TSk):
        memset("dve", v_[:], 0.0)
    dma("sp", bfbc[:].ap, b_f.rearrange("l h -> (l h)").rearrange("(o n) -> o n", o=1).to_broadcast([128, DEPTH * 8]), (), [bfbc[:]])
    dma("sp", lqa[:].ap, lq1.to_broadcast([128, DEPTH * 64]).rearrange("p (l e) -> p l e", e=64), (), [lqa[:]])
    dma("sp", lqb[:].ap, lk1.to_broadcast([128, DEPTH * 64]).rearrange("p (l e) -> p l e", e=64), (), [lqb[:]])
    tt("dve", lqa[:].ap, lqa[:].ap, lqb[:].ap, ALU.mult, [lqa[:], lqb[:]], [lqa[:]])
    P.op("dve", lambda e: e.tensor_reduce(out=ls1[:].ap, in_=lqa[:].ap, axis=AX.X, op=ALU.add), [lqa[:]], [ls1[:]])
    dma("sp", lqa[:].ap, lq2.to_broadcast([128, DEPTH * 64]).rearrange("p (l e) -> p l e", e=64), (), [lqa[:]])
    dma("sp", lqb[:].ap, lk2.to_broadcast([128, DEPTH * 64]).rearrange("p (l e) -> p l e", e=64), (), [lqb[:]])
    tt("dve", lqa[:].ap, lqa[:].ap, lqb[:].ap, ALU.mult, [lqa[:], lqb[:]], [lqa[:]])
    P.op("dve", lambda e: e.tensor_reduce(out=ls2[:].ap, in_=lqa[:].ap, axis=AX.X, op=ALU.add), [lqa[:]], [ls2[:]])
    act(ls1[:].ap, ls1[:].ap, AF.Exp, [ls1[:]], [ls1[:]])
    act(ls2[:].ap, ls2[:].ap, AF.Exp, [ls2[:]], [ls2[:]])
    tt("dve", lam[:].ap, ls1[:].ap, ls2[:].ap, ALU.subtract, [ls1[:], ls2[:]], [lam[:]])
    for l in range(DEPTH):
        tsc("dve", lam[:, l:l + 1].ap, lam[:, l:l + 1].ap, float(lambda_init(l)), None, ALU.add, None, [lam[:]], [lam[:]])
    tsc("dve", neglam[:].ap, lam[:].ap, -1.0, None, ALU.mult, None, [lam[:]], [neglam[:]])
    dma("sp", gsc[:].ap.rearrange("p (l h) -> p l h", h=4), g_diff.rearrange("l (h e) -> e l h", e=128), (), [gsc[:]], nonc=True)
    for l in range(DEPTH):
        tsc("dve", gsc[:, 4 * l:4 * l + 4].ap, gsc[:, 4 * l:4 * l + 4].ap, float(1.0 - lambda_init(l)), None, ALU.mult, None, [gsc[:]], [gsc[:]])

    slot_rr = [0]

    def wload(src2d, nch, col0, ncols, dst_view=None, c_off=0, col_off=0):
        v = dst_view
        for c in range(0, nch, 2):
            c1 = min(nch, c + 2)
            src = src2d[c * 128:c1 * 128, col0:col0 + ncols].rearrange("(c p) n -> p c n", p=128)
            dst = v[:, c_off + c:c_off + c1, col_off:col_off + ncols]
            dma("pool", dst.ap, src, (), [dst])

    def new_slot(shape):
        k = slot_rr[0]
        slot_rr[0] = (k + 1) % 4
        assert int(np.prod(shape)) * 2 <= SLOTB, shape
        return View(slots[k], 0, shape, BF16)

    def make_xT(i, src_bf, eng="dve"):
        r0, n = tiles[i]
        b = bank()
        trs = [(PSB[b][:, c * 128:c * 128 + n], src_bf.ap[0:n, c * 128:(c + 1) * 128], ident[0:n, 0:n].ap) for c in range(NCH)]
        pe_tr(trs, [src_bf, ident[:]], [psk(b)])
        src = PSB[b][:, 0:1024].rearrange("p (c t) -> p c t", t=128)[:, :, 0:n]
        dst = xT[:, :, r0:r0 + n]
        copy(eng, dst.ap, src, [], [psk(b), dst])

    def ln_stage1(i):
        r0, n = tiles[i]
        k2 = i % 4
        for k in range(2):
            zz = zacc[0:n, i, k * 512:(k + 1) * 512]
            P.op("dve", lambda e, zz=zz, k=k: e.bn_stats(out=st6[0:n, k2, k, :].ap, in_=zz.ap), [zz], [st6[:, k2, k, :]])
        P.op("dve", lambda e: e.bn_aggr(out=mv[0:n, k2, :].ap, in_=st6[0:n, k2, :, :].ap.rearrange("p a b -> p (a b)")), [st6[:, k2]], [mv[:, k2]])
        act(rstd[0:n, k2:k2 + 1].ap, mv[0:n, k2, 1:2].ap, AF.Ln, [mv[:, k2]], [rstd[:, k2:k2 + 1]], bias=LN_EPS)
        act(rstd[0:n, k2:k2 + 1].ap, rstd[0:n, k2:k2 + 1].ap, AF.Exp, [rstd[:, k2:k2 + 1]], [rstd[:, k2:k2 + 1]], scale=-0.5)

    def ln_stage2(i, gv, bv, l, final_out, make_t):
        r0, n = tiles[i]
        k2 = i % 4
        z = zacc[0:n, i, :]
        stt("dve", z.ap, z.ap, mv[0:n, k2, 0:1].ap, gv[0:n, :].ap, ALU.subtract, ALU.mult, [z, mv[:, k2], gv[:]], [z])
        stt("dve", z.ap, z.ap, rstd[0:n, k2:k2 + 1].ap, bv[0:n, :].ap, ALU.mult, ALU.add, [z, rstd[:, k2:k2 + 1], bv[:]], [z])
        if final_out:
            dma("sp", y_out[r0:r0 + n, :], z.ap, [z], [("y", i)], is_out=True)
        else:
            dma("sp", xres[r0:r0 + n, :], z.ap, [z], [("xres", i)])
        if make_t:
            xb = XBs[i % 2][0:n, :]
            copy("act", xb.ap, z.ap, [z], [xb])
            make_xT(i, xb, eng="act")

    for i, (r0, n) in enumerate(tiles):
        xb = XB[0:n, :]
        dma("pool", xb.ap, x0[r0:r0 + n, :], (), [xb])
        make_xT(i, xb)

    def phase_A(l, br, wk, wv):
        okd, ovd = (o_fk, o_fv) if br == 0 else (o_dk, o_dv)
        if br == 0:
            memset("pool", Vst[:, :, :, 64:128], 1.0)
        for i, (r0, n) in enumerate(tiles):
            for part, (wv_, od) in enumerate(((wk, okd), (wv, ovd))):
                b = bank()
                mms = [(PS[b][0:n, :], xT[:, c, r0:r0 + n].ap, wv_[:, c, :].ap, c == 0, c == NCH - 1) for c in range(NCH)]
                pe_mm(mms, [xT[:, :, r0:r0 + n], wv_[:]], [psk(b)])
                rot["ys"] = (rot["ys"] + 1) % 4
                ys = ystage[rot["ys"]][0:n, :]
                if part == 0:
                    copy("act", ys.ap, PS[b][0:n, :], [], [psk(b), ys])
                else:
                    copy("dve", ys.ap, PS[b][0:n, :], [], [psk(b), ys])
                    if br == 0:
                        vd = Vst[0:n, i, :, 0:64]
                        copy("pool", vd.ap, ys.ap.rearrange("p (g e) -> p g e", e=64), [ys], [vd])
                    else:
                        vd = VstD[0:n, i, :, :]
                        copy("pool", vd.ap, ys.ap.rearrange("p (g e) -> p g e", e=128), [ys], [vd])
                    if i == NTP:
                        copy("dve", VnewS[0:n, :].ap, ys.ap, [ys], [VnewS[0:n, :]])
                dma("sp", od[l, r0:r0 + n, :], ys.ap, [ys], [("okv", br, part, i)], is_out=True)

    def f_chain(l):
        dma("pool", Wf[:].ap, w_in[l][:, 1536:1544].rearrange("(c p) n -> p c n", p=128), (), [Wf[:]], nonc=True)
        for i, (r0, n) in enumerate(tiles):
            b = bank()
            mms = [(PS[b][0:n, 0:8], xT[:, c, r0:r0 + n].ap, Wf[:, c, :].ap, c == 0, c == NCH - 1) for c in range(NCH)]
            pe_mm(mms, [xT[:, :, r0:r0 + n], Wf[:]], [psk(b)])
            zf = Zf[0:n, i, :]
            tt("dve", zf.ap, PS[b][0:n, 0:8], bfbc[0:n, 8 * l:8 * l + 8].ap, ALU.add, [bfbc[:]], [psk(b), zf])
        tsc("dve", LFa[:].ap, Zf[:].ap, -1.0, None, ALU.mult, None, [Zf[:]], [LFa[:]])
        tt("dve", LFa[:].ap, LFa[:].ap, Zf[:].ap, ALU.min, [LFa[:], Zf[:]], [LFa[:]])
        act(LFe[:].ap, LFa[:].ap, AF.Exp, [LFa[:]], [LFe[:]])
        act(LFl[:].ap, LFe[:].ap, AF.Ln, [LFe[:]], [LFl[:]], bias=1.0)
        tsc("dve", LFa[:].ap, Zf[:].ap, 0.0, None, ALU.min, None, [Zf[:]], [LFa[:]])
        tt("dve", LF[:].ap, LFa[:].ap, LFl[:].ap, ALU.subtract, [LFa[:], LFl[:]], [LF[:]])
        dma("sp", o_lf[l, 0:TP, :].rearrange("(i p) h -> p i h", p=128), LF[:, 0:NTP, :].ap, [LF[:]], [("olf", 0)], is_out=True, nonc=True)
        dma("sp", o_lf[l, TP:T, :], LF[0:TS, NTP, :].ap, [LF[:]], [("olf", 1)], is_out=True, nonc=True)

        def cumsum(Lv, ntile, dsts, rel=False):
            memset("dve", Zex[:, 0, :], 0.0)
            for j in range(1, ntile):
                tt("dve", Zex[:, j, :].ap, Zex[:, j - 1, :].ap, Lv[:, j - 1, :].ap, ALU.add, [Zex[:, j - 1, :], Lv[:, j - 1, :]], [Zex[:, j, :]])
            b = bank()
            N = ntile * 8
            mms = [(PS[b][:, 0:N], Utri[:].ap, Lv[:, 0:ntile, :].ap.rearrange("p a b -> p (a b)"), True, False),
                   (PS[b][:, 0:N], ones32[:].ap, Zex[:, 0:ntile, :].ap.rearrange("p a b -> p (a b)"), False, not rel)]
            Rr = [Utri[:], ones32[:], Lv[:, 0:ntile, :], Zex[:, 0:ntile, :]]
            if rel:
                tt("dve", R1[:, 0, :].ap, Zex[:, ntile - 1, :].ap, Lv[:, ntile - 1, :].ap, ALU.add,
                   [Zex[:, ntile - 1, :], Lv[:, ntile - 1, :]], [R1[:, 0, :]])
                for j in range(ntile):
                    tsc("dve", R2[:, j, :].ap, R1[:, 0, :].ap, -1.0, None, ALU.mult, None, [R1[:, 0, :]], [R2[:, j, :]])
                mms.append((PS[b][:, 0:N], ones32[:].ap, R2[:, 0:ntile, :].ap.rearrange("p a b -> p (a b)"), False, True))
                Rr.append(R2[:, 0:ntile, :])
            pe_mm(mms, Rr, [psk(b)])
            cps = PS[b][:, 0:N].rearrange("p (a b) -> p a b", b=8)
            ch = Chi[:, 0:ntile, :]
            r1 = R1[:, 0:ntile, :]
            r2 = R2[:, 0:ntile, :]
            copy("dve", ch.ap, cps, [], [psk(b), ch])
            tt("dve", r1.ap, cps, ch.ap, ALU.subtract, [ch], [psk(b), r1])
            for (j0, j1, tsv, toff) in dsts:
                d0 = tsv[:, j0 - toff:j1 - toff, 0:8]
                copy("dve", d0.ap, Chi[:, j0:j1, :].ap, [Chi[:, j0:j1, :]], [d0])
            ch2 = Chi[:, 0:ntile, :]
            copy("dve", ch2.ap, r1.ap, [r1], [ch2])
            tt("dve", r2.ap, r1.ap, ch2.ap, ALU.subtract, [r1, ch2], [r2])
            for (j0, j1, tsv, toff) in dsts:
                d1 = tsv[:, j0 - toff:j1 - toff, 8:16]
                copy("dve", d1.ap, Chi[:, j0:j1, :].ap, [Chi[:, j0:j1, :]], [d1])
                d2 = tsv[:, j0 - toff:j1 - toff, 16:24]
                copy("dve", d2.ap, R2[:, j0:j1, :].ap, [R2[:, j0:j1, :]], [d2])

        cumsum(LF, NTP, [(0, NTP, TSq, 0)])
        dma("sp", Ls[:, 0:NPT, :].ap, clf[l].rearrange("(j p) h -> p j h", p=128), (), [Ls[:, 0:NPT, :]], nonc=True)
        copy("dve", Ls[0:TS, NPT, :].ap, LF[0:TS, NTP, :].ap, [LF[:]], [Ls[:, NPT, :]])
        cumsum(Ls, NPT + 1, [(0, NPT, TSk, 0), (NPT, NPT + 1, TSq, NPT - NTP)], rel=True)
        memset("pool", Rq[0:32, :], 0.0)
        dma("sp", Rq[24:27, :].ap, c_rq[:, :], (), [Rq[0:32, :]])
        for i0 in range(0, NT, 4):
            b = bank()
            trs = []
            cols = 0
            for i in range(i0, min(NT, i0 + 4)):
                r0, n = tiles[i]
                trs.append((PSB[b][0:24, (i - i0) * 128:(i - i0) * 128 + n], TSq[0:n, i, :].ap, ident[0:n, 0:n].ap))
                cols = (i - i0) * 128 + n
            pe_tr(trs, [TSq[:], ident[:]], [psk(b)])
            c0 = tiles[i0][0]
            dst = Rq[0:24, c0:c0 + cols]
            copy("dve", dst.ap, PSB[b][0:24, 0:cols], [], [psk(b), dst])

    def rq_consts_only():
        memset("pool", Rq[0:32, :], 0.0)
        dma("sp", Rq[24:27, :].ap, c_rq[:, :], (), [Rq[0:32, :]])

    def normalize_fox(acc, w, dst):
        rec = RECb[0][64:128, 0:w]
        P.op("dve", lambda e: e.reciprocal(out=rec.ap, in_=PS[acc][64:128, 0:w]), [], [psk(acc), rec])
        tt("dve", dst.ap, PS[acc][0:64, 0:w], rec.ap, ALU.mult, [rec], [psk(acc), dst])

    def sample_prefetch(l, br, bi, what="kv"):
        ck, cv = (ckf, cvf) if br == 0 else (ckd, cvd)
        kb, vb = Krb[bi % 2], Vrb[bi % 2]
        if "k" in what:
            dma("pool", kb[:].ap, ck[l, bi * 512:(bi + 1) * 512, :].rearrange("(j p) c -> p j c", p=128), (), [kb[:]])
        if "v" in what:
            dma("pool", vb[:].ap, cv[l, bi * 512:(bi + 1) * 512, :].rearrange("(j p) c -> p j c", p=128), (), [vb[:]])

    def sample_batched(l, br):
        NB = NPT // 4
        rot["mode"] = "sample"
        OUTB, SUMB = 3, 4
        dma("sp", QbS[0:32, :].ap, c_qbias[:, br * 128:(br + 1) * 128], (), [QbS[0:32, :]])
        dma("sp", DmS[0:TS, :].ap, c_dms[:, br * 128:(br + 1) * 128], (), [DmS[0:TS, :]])
        dma("sp", MaskS[:].ap, c_mask[:, br * 512:(br + 1) * 512], (), [MaskS[:]])
        if br == 1:
            for h in range(4):
                dma("sp", Gs[16 * h:16 * h + 16, :].ap, g_diff[l:l + 1, h * 128:(h + 1) * 128].to_broadcast([16, 128]), (), [Gs[0:64, :]])
            tsc("dve", Gs[0:64, :].ap, Gs[0:64, :].ap, float(1.0 - lambda_init(l)), None, ALU.mult, None, [Gs[0:64, :]], [Gs[0:64, :]])
        for k in range(2):
            memset("pool", RkTb[k][0:32, :], 0.0)

        def stage_a(bi):
            kb, rk = Krb[bi % 2], RkTb[bi % 2]
            dma("sp", rk[24:27, :].ap, c_rk[:, bi * 512:(bi + 1) * 512], (), [rk[0:32, :]])
            if br == 0:
                b = bank()
                pe_tr([(PSB[b][0:24, j * 128:(j + 1) * 128], TSk[:, bi * 4 + j, :].ap, ident[:].ap) for j in range(4)], [TSk[:], ident[:]], [psk(b)])
                copy("dve", rk[0:24, :].ap, PSB[b][0:24, 0:512], [], [psk(b), rk[0:24, :]])
            for half in range(2):
                b = bank()
                kt = KTb[2 * (bi % 2) + half]
                trs = []
                for jl in range(2):
                    for p in range(4):
                        q_ = jl * 4 + p
                        trs.append((PSB[b][:, q_ * 128:(q_ + 1) * 128], kb[:, 2 * half + jl, p * 128:(p + 1) * 128].ap, ident[:].ap))
                pe_tr(trs, [kb[:, 2 * half:2 * half + 2, :], ident[:]], [psk(b)])
                copy("act" if half else "dve", kt[:].ap, PSB[b][:, 0:1024], [], [psk(b), kt[:]])
            if bi + 2 < NB:
                sample_prefetch(l, br, bi + 2, "k")

        def stage_b(bi):
            rk = RkTb[bi % 2]
            st = stbank()
            for jl in range(4):
                kt = KTb[2 * (bi % 2) + jl // 2]
                mms = []
                for p in range(4):
                    q_ = (jl % 2) * 4 + p
                    mms.append((PS[st][:, jl * 128:(jl + 1) * 128], kt[:, q_ * 128:(q_ + 1) * 128].ap, QblkS[:, p, :].ap, p == 0, False))
                mms.append((PS[st][:, jl * 128:(jl + 1) * 128], rk[0:32, jl * 128:(jl + 1) * 128].ap, QbS[0:32, :].ap, False, True))
                pe_mm(mms, [kt[:], QblkS[:], rk[0:32, :], QbS[0:32, :]], [psk(st)])
            pt = PTn[bi % 2]
            act(pt[:].ap, PS[st][:, :], AF.Exp, [], [psk(st), pt[:]])

        def stage_c(bi):
            vb, pt = Vrb[bi % 2], PTn[bi % 2]
            pe_mm([(PS[OUTB][:, :], pt[:, jl * 128:(jl + 1) * 128].ap, vb[:, jl, :].ap, bi == 0 and jl == 0, False) for jl in range(4)],
                  [pt[:], vb[:]], [psk(OUTB)])
            pe_mm([(PS[SUMB][:, 0:8], pt[:, jl * 128:(jl + 1) * 128].ap, onesb[:, 0:8].ap, bi == 0 and jl == 0, False) for jl in range(4)],
                  [pt[:], onesb[:]], [psk(SUMB)])
            if bi + 2 < NB:
                sample_prefetch(l, br, bi + 2, "v")

        stage_a(0)
        for bi in range(NB):
            if bi + 1 < NB:
                stage_a(bi + 1)
            stage_b(bi)
            if bi >= 1:
                stage_c(bi - 1)
        stage_c(NB - 1)
        st = stbank()
        mms = [(PS[st][0:TS, 0:128], KnewS[:, p, :].ap, QblkS[:, p, :].ap, p == 0, False) for p in range(4)]
        mms.append((PS[st][0:TS, 0:128], Rq[0:32, TP:T].ap, QbS[0:32, :].ap, False, False))
        mms.append((PS[st][0:TS, 0:128], ident[0:TS, 0:TS].ap, DmS[0:TS, :].ap, False, True))
        pe_mm(mms, [KnewS[:], QblkS[:], Rq[0:32, TP:T], QbS[0:32, :], ident[:], DmS[0:TS, :]], [psk(st)])
        ptn = PTn[NB % 2][0:TS, 0:128]
        act(ptn.ap, PS[st][0:TS, 0:128], AF.Exp, [], [psk(st), ptn])
        pe_mm([(PS[OUTB][:, :], ptn.ap, VnewS[0:TS, :].ap, False, True)], [ptn, VnewS[0:TS, :]], [psk(OUTB)])
        pe_mm([(PS[SUMB][:, 0:8], ptn.ap, onesb[0:TS, 0:8].ap, False, True)], [ptn, onesb[:]], [psk(SUMB)])
        P.op("dve", lambda e: e.reciprocal(out=rsS[:, 0:1].ap, in_=PS[SUMB][:, 0:1]), [], [psk(SUMB), rsS[:]])
        tt("dve", Zm[:].ap, PS[OUTB][:, :], MaskS[:].ap, ALU.mult, [MaskS[:]], [psk(OUTB), Zm[:]])
        P.op("dve", lambda e: e.tensor_reduce(out=Zs[:].ap, in_=Zm[:].ap.rearrange("p (a x) -> p x a", a=4), axis=AX.X, op=ALU.add), [Zm[:]], [Zs[:]])
        if br == 0:
            b = bank()
            tsc("dve", Znb[:].ap, Zs[:].ap, rsS[:, 0:1].ap, None, ALU.mult, None, [Zs[:], rsS[:]], [Znb[:]])
            pe_tr([(PSB[b][:, 0:128], Znb[:].ap, ident[:].ap)], [Znb[:], ident[:]], [psk(b)])
            for hl in range(2):
                dst = oaT[64 * hl:64 * hl + 64, :, TP:T]
                src = PSB[b][64 * hl:64 * hl + 64, 0:128].rearrange("q (p hl t) -> q p hl t", hl=2, t=TS)[:, :, hl, :]
                copy("dve", dst.ap, src, [], [psk(b), dst])
        else:
            tsc("dve", Znb[0:64, :].ap, Zs[0:64, :].ap, rsS[0:64, 0:1].ap, None, ALU.mult, None, [Zs[:], rsS[:]], [Znb[0:64, :]])
            tsc("dve", Znb[64:128, :].ap, Zs[64:128, :].ap, rsS[64:128, 0:1].ap, neglam[64:128, l:l + 1].ap, ALU.mult, ALU.mult,
                [Zs[:], rsS[:], neglam[:]], [Znb[64:128, :]])
            b = bank()
            pe_mm([(PS[b][0:64, 0:128], ident[:, 0:64].ap, Znb[:].ap, True, False),
                   (PS[b][0:64, 0:128], ident[:, 64:128].ap, Znb[:].ap, False, True)], [ident[:], Znb[:]], [psk(b)])
            act(SQs[0:64, :].ap, PS[b][0:64, 0:128], AF.Square, [], [psk(b), SQs[0:64, :], ssS[0:64, 0:1]], accum_out=ssS[0:64, 0:1].ap)
            act(ssS[0:64, 1:2].ap, ssS[0:64, 0:1].ap, AF.Ln, [ssS[0:64, 0:1]], [ssS[0:64, 1:2]], scale=1.0 / 128.0, bias=RMS_EPS)
            act(ssS[0:64, 1:2].ap, ssS[0:64, 1:2].ap, AF.Exp, [ssS[0:64, 1:2]], [ssS[0:64, 1:2]], scale=-0.5)
            stt("dve", Znb2[0:64, :].ap, PS[b][0:64, 0:128], ssS[0:64, 1:2].ap, Gs[0:64, :].ap, ALU.mult, ALU.mult,
                [ssS[0:64, 1:2], Gs[0:64, :]], [psk(b), Znb2[0:64, :]])
            b2 = bank()
            pe_tr([(PSB[b2][:, 0:64], Znb2[0:64, :].ap, ident[0:64, 0:64].ap)], [Znb2[0:64, :], ident[:]], [psk(b2)])
            dst = obT[:, :, TP:T]
            copy("dve", dst.ap, PSB[b2][:, 0:64].rearrange("q (h t) -> q h t", t=TS), [], [psk(b2), dst])
        rot["mode"] = "proj"

    def phase_B(l, br, wq, wk):
        ck, cv = (ckf, cvf) if br == 0 else (ckd, cvd)
        QA, QB, KA, KB = QKt
        rot["mode"] = "proj"
        memset("pool", QblkS[:], 0.0)
        for bi in range(min(2, NPT // 4)):
            sample_prefetch(l, br, bi)
        for p in range(4):
            Aq = Amat[0:32, br * 8 + p, :]
            Ak = Amat[0:32, br * 8 + 4 + p, :]
            for g, (c0, w) in enumerate(groups):
                xs = xT[:, :, c0:c0 + w]
                bq = bank()
                pe_mm([(PS[bq][:, 0:w], wq[:, c, p * 128:(p + 1) * 128].ap, xT[:, c, c0:c0 + w].ap, c == 0, c == NCH - 1) for c in range(NCH)],
                      [xs, wq[:]], [psk(bq)])
                act(QA[0:64, c0:c0 + w].ap, PS[bq][0:64, 0:w], AF.Copy, [], [psk(bq), QA[0:64, c0:c0 + w]], scale=0.125)
                act(QB[0:64, c0:c0 + w].ap, PS[bq][64:128, 0:w], AF.Copy, [], [psk(bq), QB[0:64, c0:c0 + w]], scale=0.125)
                bk = bank()
                pe_mm([(PS[bk][:, 0:w], wk[:, c, p * 128:(p + 1) * 128].ap, xT[:, c, c0:c0 + w].ap, c == 0, c == NCH - 1) for c in range(NCH)],
                      [xs, wk[:]], [psk(bk)])
                copy("dve", KA[0:64, c0:c0 + w].ap, PS[bk][0:64, 0:w], [], [psk(bk), KA[0:64, c0:c0 + w]])
                copy("dve", KB[0:64, c0:c0 + w].ap, PS[bk][64:128, 0:w], [], [psk(bk), KB[0:64, c0:c0 + w]])
                ba = bank()
                rqs = Rq[0:32, c0:c0 + w]
                pe_mm([(PS[ba][0:64, 0:w], Aq.ap, rqs.ap, True, True), (PS[ba][64:128, 0:w], Ak.ap, rqs.ap, True, True)],
                      [Amat[0:32], rqs], [psk(ba)])
                copy("act", QA[64:96, c0:c0 + w].ap, PS[ba][0:32, 0:w], [], [psk(ba), QA[64:96, c0:c0 + w]])
                copy("act", QB[64:96, c0:c0 + w].ap, PS[ba][32:64, 0:w], [], [psk(ba), QB[64:96, c0:c0 + w]])
                copy("dve", KA[64:96, c0:c0 + w].ap, PS[ba][64:96, 0:w], [], [psk(ba), KA[64:96, c0:c0 + w]])
                copy("dve", KB[64:96, c0:c0 + w].ap, PS[ba][96:128, 0:w], [], [psk(ba), KB[64:96, c0:c0 + w]])
            if br == 0 and "qa" in dbg_names and p == 0 and l == 0:
                dbg("qa", QA[0:96, :], [96, T])
                dbg("ka", KA[0:96, :], [96, T])
            for hh, (Qh_, Kh_) in enumerate(((QA, KA), (QB, KB))):
                cb = (2 * p + hh) * 16 if br == 0 else hh * 64 + p * 16
                dq = QblkS[64 * hh:64 * hh + 64, p, cb:cb + 16]
                copy("dve", dq.ap, Qh_[0:64, TP:T].ap, [Qh_[0:64, TP:T]], [dq])
                dk = KnewS[64 * hh:64 * hh + 64, p, :]
                copy("dve", dk.ap, Kh_[0:64, TP:T].ap, [Kh_[0:64, TP:T]], [dk])
            heads = ((QA, KA), (QB, KB))
            dmi = 0 if br == 0 else 1 + p
            nacc = 1 if br == 0 else 2

            def vlhs(hh, a, tile_j, rows=128, p=p):
                if br == 0:
                    return Vst[0:rows, tile_j, 2 * p + hh, :]
                return VstD[0:rows, tile_j, p, :] if a == 0 else onesb[0:rows, :]
            pending_fin = []
            for m in range(NGP):
                jmax = 4 * m + 3
                accs_h = []
                for hh, (Qh, Kh) in enumerate(heads):
                    accs = []
                    for a in range(nacc):
                        rot["acc"] = (rot["acc"] + 1) % 4
                        accs.append(ACC_BANKS[rot["acc"]])
                    accs_h.append(accs)
                    pend = []
                    for j in range(jmax + 3):
                        cur = None
                        if j <= jmax:
                            cs = max(0, 128 * (j - 4 * m))
                            w = 512 - cs
                            st = stbank()
                            ks = Kh[0:96, 128 * j:128 * j + 128]
                            qs = Qh[0:96, 512 * m + cs:512 * m + 512]
                            mms = [(PS[st][:, cs:512], ks.ap, qs.ap, True, j < 4 * m)]
                            R = [ks, qs]
                            if j >= 4 * m:
                                mms.append((PS[st][:, cs:cs + 128], ident[:].ap, dm[:, dmi, :].ap, False, True))
                                R += [ident[:], dm[:]]
                            pe_mm(mms, R, [psk(st)])
                            pt = ptbuf()
                            pts = pt[:, cs:512]
                            act(pts.ap, PS[st][:, cs:512], AF.Exp, [], [psk(st), pts])
                            cur = (j, cs, pts)
                        if cur is not None:
                            pend.append(cur)
                        if pend and (len(pend) > 2 or cur is None):
                            jj, cs2, pts2 = pend.pop(0)
                            for a in range(nacc):
                                vs = vlhs(hh, a, jj)
                                pe_mm([(PS[accs[a]][:, cs2:512], vs.ap, pts2.ap, jj == 0, jj == jmax)], [vs, pts2], [psk(accs[a])])
                    c0 = 512 * m
                    if pending_fin:
                        pending_fin.pop(0)()
                    if br == 0:
                        normalize_fox(accs[0], 512, oaT[64 * hh:64 * hh + 64, p, c0:c0 + 512])
                    else:
                        diff_norm(hh, accs, 512)
                if br == 1:
                    pending_fin.append(lambda l=l, p=p, m=m: diff_finish(l, p, 512 * m, 512))
            while pending_fin:
                pending_fin.pop(0)()

        sample_batched(l, br)

    def diff_norm(c, accs, w):
        rec = RECb[c][:, 0:w]
        if c == 0:
            P.op("dve", lambda e: e.reciprocal(out=rec.ap, in_=PS[accs[1]][:, 0:w]), [], [psk(accs[1]), rec])
        else:
            act(rec.ap, PS[accs[1]][:, 0:w], AF.Ln, [], [psk(accs[1]), rec])
            act(rec.ap, rec.ap, AF.Exp, [rec], [rec], scale=-1.0)
        oc = OCb[c][:, 0:w]
        tt("dve", oc.ap, PS[accs[0]][:, 0:w], rec.ap, ALU.mult, [rec], [psk(accs[0]), oc])

    def diff_finish(l, h, c0, w):
        ob = OCb[2][:, 0:w]
        stt("dve", ob.ap, OCb[1][:, 0:w].ap, neglam[:, l:l + 1].ap, OCb[0][:, 0:w].ap, ALU.mult, ALU.add,
            [OCb[1][:, 0:w], OCb[0][:, 0:w], neglam[:]], [ob])
        sq = SQb[:, 0:w]
        tt("dve", sq.ap, ob.ap, ob.ap, ALU.mult, [ob], [sq])
        old_mode = rot.get("mode", "all")
        rot["mode"] = "sample"
        b = bank()
        rot["mode"] = old_mode
        pe_mm([(PS[b][:, 0:w], onesdiv[:].ap, sq.ap, True, True)], [onesdiv[:], sq], [psk(b)])
        rs = RSb[:, 0:w]
        act(rs.ap, PS[b][:, 0:w], AF.Ln, [], [psk(b), rs], bias=RMS_EPS)
        act(rs.ap, rs.ap, AF.Exp, [rs], [rs], scale=-0.5)
        dst = obT[:, h, c0:c0 + w]
        stt("dve", dst.ap, ob.ap, gsc[:, 4 * l + h:4 * l + h + 1].ap, rs.ap, ALU.mult, ALU.mult, [ob, rs, gsc[:]], [dst])

    def phase_C(l, q4, wgv, wbv):
        for g, (c0, w) in enumerate(groups):
            for k in range(2):
                nchk = 2 * q4 + k
                res = []
                for bi, src in enumerate((oaT, obT)):
                    b1 = bank()
                    pe_mm([(PS[b1][:, 0:w], wgv[:, c, bi * 256 + k * 128:bi * 256 + (k + 1) * 128].ap, xT[:, c, c0:c0 + w].ap, c == 0, c == NCH - 1) for c in range(NCH)],
                          [xT[:, :, c0:c0 + w], wgv[:]], [psk(b1)])
                    sg = CS[bi][:, 0:w]
                    act(sg.ap, PS[b1][:, 0:w], AF.Sigmoid, [], [psk(b1), sg])
                    b2 = bank()
                    pe_mm([(PS[b2][:, 0:w], wbv[:, bi * 4 + c, k * 128:(k + 1) * 128].ap, src[:, c, c0:c0 + w].ap, c == 0, c == 3) for c in range(4)],
                          [src[:, :, c0:c0 + w], wbv[:]], [psk(b2)])
                    t_ = CS[2 + bi][:, 0:w]
                    tt("dve", t_.ap, PS[b2][:, 0:w], sg.ap, ALU.mult, [sg], [psk(b2), t_])
                    res.append(t_)
                dst = mergedT[:, nchk, c0:c0 + w]
                tt("pool", dst.ap, res[0].ap, res[1].ap, ALU.add, res, [dst])

    def phase_D(l, wo0, wo1):
        dma("sp", LNG[:].ap, ln1_g[l:l + 1, :].to_broadcast([128, D]), (), [LNG[:]])
        dma("sp", LNB[:].ap, ln1_b[l:l + 1, :].to_broadcast([128, D]), (), [LNB[:]])
        pend = {}
        for it in range(NT + 3):
            if it - 3 >= 0 and it - 3 < NT:
                ln_stage2(it - 3, LNG, LNB, l, False, True)
            if it < NT:
                i = it
                r0, n = tiles[i]
                xr = xrow[i % 2][0:n, :]
                if l == 0:
                    dma("sp", xr.ap, x0[r0:r0 + n, :], (), [xr])
                else:
                    dma("sp", xr.ap, xres[r0:r0 + n, :], [("xres", i)], [xr])
                bs = []
                for nh, wv_ in enumerate((wo0, wo1)):
                    b = bank()
                    pe_mm([(PS[b][0:n, :], mergedT[:, c, r0:r0 + n].ap, wv_[:, c, :].ap, c == 0, c == NCH - 1) for c in range(NCH)],
                          [mergedT[:, :, r0:r0 + n], wv_[:]], [psk(b)])
                    bs.append(b)
                pend[i] = bs
            if it - 1 >= 0 and it - 1 < NT:
                i = it - 1
                r0, n = tiles[i]
                xr = xrow[i % 2][0:n, :]
                for nh in range(2):
                    b = pend[i][nh]
                    zz = zacc[0:n, i, nh * 512:(nh + 1) * 512]
                    stt("dve", zz.ap, xrow[i % 2][0:n, nh * 512:(nh + 1) * 512].ap, ALPHA, PS[b][0:n, :], ALU.mult, ALU.add, [xr], [psk(b), zz])
                ln_stage1(i)

    def phase_E1(l, nh, wpg, wpp):
        for i, (r0, n) in enumerate(tiles):
            ptt = pTt[i % 2]
            if nh == 0:
                pk = ptok[i % 2][0:n, :]
                dma("pool", pk.ap, p_in[l, r0:r0 + n, :], (), [pk])
                b = bank()
                pe_tr([(PSB[b][:, c * 128:c * 128 + n], pk.ap[0:n, c * 128:(c + 1) * 128], ident[0:n, 0:n].ap) for c in range(2)], [pk, ident[:]], [psk(b)])
                src = PSB[b][:, 0:256].rearrange("p (c t) -> p c t", t=128)[:, :, 0:n]
                copy("dve", ptt[:, :, 0:n].ap, src, [], [psk(b), ptt[:, :, 0:n]])
            else:
                pk = ptok[i % 2][0:n, :]
                dma("pool", pk.ap, p_in[l, r0:r0 + n, :], (), [pk])
                b = bank()
                pe_tr([(PSB[b][:, c * 128:c * 128 + n], pk.ap[0:n, c * 128:(c + 1) * 128], ident[0:n, 0:n].ap) for c in range(2)], [pk, ident[:]], [psk(b)])
                src = PSB[b][:, 0:256].rearrange("p (c t) -> p c t", t=128)[:, :, 0:n]
                copy("dve", ptt[:, :, 0:n].ap, src, [], [psk(b), ptt[:, :, 0:n]])
            b1 = bank()
            pe_mm([(PS[b1][0:n, :], xT[:, c, r0:r0 + n].ap, wpg[:, c, :].ap, c == 0, c == NCH - 1) for c in range(NCH)],
                  [xT[:, :, r0:r0 + n], wpg[:]], [psk(b1)])
            sg = SGe[i % 2][0:n, :]
            act(sg.ap, PS[b1][0:n, :], AF.Sigmoid, [], [psk(b1), sg])
            b2 = bank()
            pe_mm([(PS[b2][0:n, :], ptt[:, c, 0:n].ap, wpp[:, c, :].ap, c == 0, c == 1) for c in range(2)], [ptt[:, :, 0:n], wpp[:]], [psk(b2)])
            pl = PLe[0][0:n, :]
            tt("dve", pl.ap, PS[b2][0:n, :], sg.ap, ALU.mult, [sg], [psk(b2), pl])
            zz = zacc[0:n, i, nh * 512:(nh + 1) * 512]
            stt("dve", zz.ap, zz.ap, ALPHA, pl.ap, ALU.mult, ALU.add, [zz, pl], [zz])

    def phase_E2gu(l, fw, wgv, wuv):
        nfc = fw // 128
        for fc in range(nfc):
            for g, (c0, w) in enumerate(groups):
                b1 = bank()
                pe_mm([(PS[b1][:, 0:w], wgv[:, c, fc * 128:(fc + 1) * 128].ap, xT[:, c, c0:c0 + w].ap, c == 0, c == NCH - 1) for c in range(NCH)],
                      [xT[:, :, c0:c0 + w], wgv[:]], [psk(b1)])
                sl = SLe[(fc * NG + g) % 2][:, 0:w]
                act(sl.ap, PS[b1][:, 0:w], AF.Silu, [], [psk(b1), sl])
                b2 = bank()
                pe_mm([(PS[b2][:, 0:w], wuv[:, c, fc * 128:(fc + 1) * 128].ap, xT[:, c, c0:c0 + w].ap, c == 0, c == NCH - 1) for c in range(NCH)],
                      [xT[:, :, c0:c0 + w], wuv[:]], [psk(b2)])
                dst = hT[:, fc, c0:c0 + w]
                tt("dve", dst.ap, PS[b2][:, 0:w], sl.ap, ALU.mult, [sl], [psk(b2), dst])

    def phase_E2d(l, fw, wdv, last):
        nfc = fw // 128
        if last:
            dma("sp", LNG[:].ap, ln2_g[l:l + 1, :].to_broadcast([128, D]), (), [LNG[:]])
            dma("sp", LNB[:].ap, ln2_b[l:l + 1, :].to_broadcast([128, D]), (), [LNB[:]])
        fo, mk = l == DEPTH - 1, l < DEPTH - 1
        pend = {}
        for it in range(NT + 3):
            if last and 0 <= it - 3 < NT:
                ln_stage2(it - 3, LNG, LNB, l, fo, mk)
            if it < NT:
                i = it
                r0, n = tiles[i]
                bs = []
                for nh in range(2):
                    b = bank()
                    pe_mm([(PS[b][0:n, :], hT[:, fc, r0:r0 + n].ap, wdv[:, fc, nh * 512:(nh + 1) * 512].ap, fc == 0, fc == nfc - 1) for fc in range(nfc)],
                          [hT[:, 0:nfc, r0:r0 + n], wdv[:]], [psk(b)])
                    bs.append(b)
                pend[i] = bs
            if 0 <= it - 1 < NT:
                i = it - 1
                r0, n = tiles[i]
                for nh in range(2):
                    b = pend[i][nh]
                    zz = zacc[0:n, i, nh * 512:(nh + 1) * 512]
                    tt("dve", zz.ap, PS[b][0:n, :], zz.ap, ALU.add, [zz], [psk(b), zz])
                if last:
                    ln_stage1(i)

    steps = []

    def add_step(loads, body):
        steps.append((loads, body))

    for l in range(DEPTH):
        W = w_in[l]
        add_step([("dk", W, 8, 2056, 512), ("dv", W, 8, 2568, 512)],
                 lambda v, l=l: (rq_consts_only(), phase_A(l, 1, v["dk"], v["dv"])))
        add_step([("q", W, 8, 1544, 512), ("k", W, 8, 2056, 512)],
                 lambda v, l=l: phase_B(l, 1, v["q"], v["k"]))
        add_step([("fk", W, 8, 512, 512), ("fv", W, 8, 1024, 512)],
                 lambda v, l=l: (f_chain(l), phase_A(l, 0, v["fk"], v["fv"])))
        add_step([("q", W, 8, 0, 512), ("k", W, 8, 512, 512)],
                 lambda v, l=l: phase_B(l, 0, v["q"], v["k"]))
        for q4 in range(4):
            add_step([("g", None, q4), ("b", None, q4)], lambda v, l=l, q4=q4: phase_C(l, q4, v["g"], v["b"]))
        add_step([("o0", w_o[l], 8, 0, 512), ("o1", w_o[l], 8, 512, 512)], lambda v, l=l: phase_D(l, v["o0"], v["o1"]))
        for nh in range(2):
            add_step([("pg", w_pg[l], 8, nh * 512, 512), ("pp", w_pp[l], 2, nh * 512, 512)],
                     lambda v, l=l, nh=nh: phase_E1(l, nh, v["pg"], v["pp"]))
        f0 = 0
        fparts = []
        while f0 < DFF:
            fw = min(512, DFF - f0)
            fparts.append((f0, fw))
            f0 += fw
        for k, (f0, fw) in enumerate(fparts):
            add_step([("wg", w_g[l], 8, f0, fw), ("wu", w_u[l], 8, f0, fw)],
                     lambda v, l=l, fw=fw: phase_E2gu(l, fw, v["wg"], v["wu"]))
            add_step([("wd", w_d[l][f0:f0 + fw, :], fw // 128, 0, D)],
                     lambda v, l=l, fw=fw, last=(k == len(fparts) - 1): phase_E2d(l, fw, v["wd"], last))

    def do_loads(loads, l):
        views = {}
        for spec in loads:
            nm = spec[0]
            if nm == "g":
                q4 = spec[2]
                v = new_slot((8, 512))
                wload(w_in[l], 8, 3080 + q4 * 256, 256, v, 0, 0)
                wload(w_in[l], 8, 4104 + q4 * 256, 256, v, 0, 256)
            elif nm == "b":
                q4 = spec[2]
                v = new_slot((8, 256))
                wload(w_ba[l], 4, q4 * 256, 256, v, 0, 0)
                wload(w_bb[l], 4, q4 * 256, 256, v, 4, 0)
            else:
                _, src, nch, col0, ncols = spec
                v = new_slot((nch, ncols))
                wload(src, nch, col0, ncols, v, 0, 0)
            views[nm] = v
        return views

    nsteps_per_layer = len(steps) // DEPTH
    pending = do_loads(steps[0][0], 0)
    for k, (loads, body) in enumerate(steps):
        cur = pending
        if k + 1 < len(steps):
            nl = len(steps[k + 1][0])
            cur_slots = {v.buf for v in cur.values()}
            tries = 0
            while tries < 4 and any(slots[(slot_rr[0] + t) % 4] in cur_slots for t in range(nl)):
                slot_rr[0] = (slot_rr[0] + 1) % 4
                tries += 1
            pending = do_loads(steps[k + 1][0], (k + 1) // nsteps_per_layer)
        rot["mode"] = "all"
        body(cur)

    fin = P.op("sp", None, (), ())
    P.ops[fin].deps = set(out_dmas)

    with ExitStack() as stack:
        streams = P.emit(nc, stack)
        with nc.Block() as block:
            @block.tensor
            def _(e):
                replay(e, streams["pe"])

            @block.scalar
            def _(e):
                replay(e, streams["act"])

            @block.vector
            def _(e):
                replay(e, streams["dve"])

            @block.gpsimd
            def _(e):
                replay(e, streams["pool"])

            @block.sync
            def _(e):
                replay(e, streams["sp"])
    return nc, P.stats


WNAMES = ["w_in", "b_forget", "diff_norm_g", "w_branch_fox", "w_branch_diff", "w_out", "ln1_g", "ln1_b",
          "w_ffn_gate", "w_ffn_up", "w_ffn_down", "w_ple_gate", "w_ple_proj", "ln2_g", "ln2_b"]


def run(cfg, inputs, ncores, trace=False):
    TP, PAST, DEPTH = cfg["TP"], cfg["PAST"], cfg["DEPTH"]
    T = TP + TS
    nc, stats = build(cfg)
    consts = make_consts(TP, PAST)
    f = lambda a: np.ascontiguousarray(np.asarray(a, dtype=np.float32))
    shared = {k: f(inputs[k]) for k in WNAMES}
    for k in ("lambda_q1", "lambda_k1", "lambda_q2", "lambda_k2"):
        shared[k] = f(inputs[k]).reshape(1, DEPTH * 64)
    shared.update(consts)
    xp, xs = f(inputs["x_prompt"]), f(inputs["x_sample"])
    pp, ps_ = f(inputs["p_prompt"]), f(inputs["p_sample"])
    in_maps = []
    for b in range(ncores):
        m = dict(shared)
        m["x0"] = np.concatenate([xp[b], xs[b]], axis=0)
        m["p"] = np.concatenate([pp[:, b], ps_[:, b]], axis=1)
        m["ckf"] = f(inputs["cache_fox_k"][:, b]).reshape(DEPTH, PAST, 512)
        m["cvf"] = f(inputs["cache_fox_v"][:, b]).reshape(DEPTH, PAST, 512)
        m["clf"] = f(inputs["cache_fox_logf"][:, b]).reshape(DEPTH, PAST, 8)
        m["ckd"] = f(inputs["cache_diff_k"][:, b]).reshape(DEPTH, PAST, 512)
        m["cvd"] = f(inputs["cache_diff_v"][:, b]).reshape(DEPTH, PAST, 512)
        in_maps.append(m)
    res = run_bass_kernel_spmd(nc, in_maps, core_ids=list(range(ncores)), trace=trace)
    rs = res.results
    B = ncores
    st = lambda k: np.stack([np.asarray(r[k]) for r in rs], axis=0)
    y = st("y")
    fk, fv, lf, dk, dv = st("o_fk"), st("o_fv"), st("o_lf"), st("o_dk"), st("o_dv")
    tr = lambda a: np.ascontiguousarray(np.moveaxis(a, 0, 1))
    fk, fv, lf, dk, dv = tr(fk), tr(fv), tr(lf), tr(dk), tr(dv)
    outs = (
        y[:, :TP], y[:, TP:],
        fk[:, :, :TP].reshape(DEPTH, B, TP, 8, 64), fv[:, :, :TP].reshape(DEPTH, B, TP, 8, 64), lf[:, :, :TP],
        dk[:, :, :TP].reshape(DEPTH, B, TP, 4, 2, 64), dv[:, :, :TP].reshape(DEPTH, B, TP, 4, 128),
        fk[:, :, TP:].reshape(DEPTH, B, TS, 8, 64), fv[:, :, TP:].reshape(DEPTH, B, TS, 8, 64), lf[:, :, TP:],
        dk[:, :, TP:].reshape(DEPTH, B, TS, 4, 2, 64), dv[:, :, TP:].reshape(DEPTH, B, TS, 4, 128),
    )
    outs = tuple(np.ascontiguousarray(o, dtype=np.float32) for o in outs)
    dbgs = {k[4:]: st(k) for k in rs[0].keys() if k.startswith("dbg_")}
    return outs, res, stats, dbgs


def kernel(**inputs):
    cfg = dict(TP=2048, PAST=4096, DEPTH=4)
    outs, _, _, _ = run(cfg, inputs, 8)
    return outs
```

```python
import itertools
import math
from contextlib import ExitStack

import numpy as np
import ml_dtypes
import concourse.bass as bass
import concourse.mybir as mybir
from concourse.bass_utils import run_bass_kernel_spmd

F32 = mybir.dt.float32
BF16 = mybir.dt.bfloat16
AF = mybir.ActivationFunctionType
ALU = mybir.AluOpType
AX = mybir.AxisListType

D = 1024
NCH = 8
DFF = 2816
PLE = 256
TS = 16
INC = 5128
ALPHA = 8.0 ** 0.25
LN_EPS = 1e-5
RMS_EPS = 1e-5
CELL = 32
SEG = 4000
NPOOL = 16


def lambda_init(l):
    return 0.8 - 0.6 * math.exp(-0.3 * l)


ENGS = ("pe", "act", "dve", "pool", "sp")
EIDX = {e: i for i, e in enumerate(ENGS)}


class Buf:
    def __init__(self, nc, name, nbytes):
        nbytes = (nbytes + 63) // 64 * 64
        self.name = name
        self.nbytes = nbytes
        self.t = nc.alloc_sbuf_tensor(name, [128, nbytes // 2], BF16)
        nc_ = (nbytes + CELL - 1) // CELL
        self.last_w = np.full(nc_, -1, dtype=np.int64)
        self.last_r = np.full((len(ENGS), nc_), -1, dtype=np.int64)


class Sub:
    __slots__ = ("ap", "buf", "ivs")

    def __init__(self, ap, buf, ivs):
        self.ap, self.buf, self.ivs = ap, buf, ivs


class View:
    def __init__(self, buf, off, shape, dtype):
        self.buf, self.off, self.shape, self.dtype = buf, off, tuple(shape), dtype
        self.es = 4 if dtype == F32 else 2
        n = int(np.prod(shape))
        assert off % 4 == 0 and off + n * self.es <= buf.nbytes, (buf.name, off, shape, buf.nbytes)
        base = buf.t[:, off // 2: off // 2 + n * self.es // 2]
        if dtype == F32:
            base = base.bitcast(F32)
        if len(shape) == 2:
            base = base.rearrange("p (a b) -> p a b", b=shape[1])
        elif len(shape) == 3:
            base = base.rearrange("p (a b c) -> p a b c", b=shape[1], c=shape[2])
        self.ap = base
        self.strides = [int(np.prod(shape[k + 1:])) for k in range(len(shape))]

    def __getitem__(self, idx):
        if not isinstance(idx, tuple):
            idx = (idx,)
        fidx = list(idx[1:]) + [slice(None)] * (len(self.shape) - len(idx) + 1)
        ap = self.ap[(idx[0], *fidx)]
        rngs = []
        for d, ix in zip(self.shape, fidx):
            if isinstance(ix, int):
                rngs.append((ix, ix + 1))
            else:
                s, e, _ = ix.indices(d)
                rngs.append((s, e))
        ivs = []
        k = len(rngs) - 1
        while k > 0 and rngs[k] == (0, self.shape[k]):
            k -= 1
        for o in itertools.product(*[range(a, b) for a, b in rngs[:k]]):
            base = sum(i * st for i, st in zip(o, self.strides[:k]))
            lo = self.off + (base + rngs[k][0] * self.strides[k]) * self.es
            hi = self.off + (base + rngs[k][1] * self.strides[k]) * self.es
            ivs.append((lo, hi))
        return Sub(ap, self.buf, ivs)


class Op:
    __slots__ = ("eng", "fn", "deps", "dma", "needs_inc", "sem", "val", "semkey", "clock")

    def __init__(self, eng, fn, deps, dma):
        self.eng, self.fn, self.deps, self.dma = eng, fn, deps, dma
        self.needs_inc = dma
        self.sem = self.val = self.semkey = self.clock = None


class Prog:
    def __init__(self):
        self.ops = []
        self.kw = {}
        self.kr = {}

    def op(self, eng, fn, R=(), W=(), dma=False):
        idx = len(self.ops)
        e = EIDX[eng]
        deps = set()
        for r in R:
            if isinstance(r, Sub):
                for lo, hi in r.ivs:
                    c0, c1 = lo // CELL, (hi + CELL - 1) // CELL
                    deps.update(np.unique(r.buf.last_w[c0:c1]).tolist())
            else:
                w = self.kw.get(r)
                if w is not None:
                    deps.add(w)
        for w_ in W:
            if isinstance(w_, Sub):
                for lo, hi in w_.ivs:
                    c0, c1 = lo // CELL, (hi + CELL - 1) // CELL
                    deps.update(np.unique(w_.buf.last_w[c0:c1]).tolist())
                    deps.update(np.unique(w_.buf.last_r[:, c0:c1]).tolist())
            else:
                w = self.kw.get(w_)
                if w is not None:
                    deps.add(w)
                deps.update(self.kr.get(w_, {}).values())
        for r in R:
            if isinstance(r, Sub):
                for lo, hi in r.ivs:
                    c0, c1 = lo // CELL, (hi + CELL - 1) // CELL
                    r.buf.last_r[e, c0:c1] = idx
            else:
                self.kr.setdefault(r, {})[eng] = idx
        for w_ in W:
            if isinstance(w_, Sub):
                for lo, hi in w_.ivs:
                    c0, c1 = lo // CELL, (hi + CELL - 1) // CELL
                    w_.buf.last_w[c0:c1] = idx
                    w_.buf.last_r[:, c0:c1] = -1
            else:
                self.kw[w_] = idx
                self.kr[w_] = {}
        deps.discard(-1)
        if eng == "pe" and not dma:
            deps = {d for d in deps if not (self.ops[d].eng == "pe" and not self.ops[d].dma)}
        self.ops.append(Op(eng, fn, deps, dma))
        return idx

    def emit(self, nc, stack):
        ops = self.ops
        for o in ops:
            for d in o.deps:
                ops[d].needs_inc = True
        ninc = {e: 0 for e in ENGS}
        for o in ops:
            if o.needs_inc and not o.dma:
                ninc[o.eng] += 1
        esems = {e: [stack.enter_context(nc.semaphore(f"s_{e}_{k}")) for k in range(ninc[e] // SEG + 1)]
                 for e in ENGS}
        dsems = {e: [stack.enter_context(nc.semaphore(f"d_{e}_{k}")) for k in range(NPOOL)]
                 for e in ("sp", "pool")}
        known = {e: {} for e in ENGS}
        cnt = {e: 0 for e in ENGS}
        dcnt = {e: 0 for e in ENGS}
        dhist = {e: [] for e in ENGS}
        streams = {e: [] for e in ENGS}
        nwait = 0
        for o in ops:
            kn = known[o.eng]
            st = streams[o.eng]

            def need(dop):
                nonlocal nwait
                if kn.get(dop.semkey, 0) < dop.val:
                    st.append(("w", dop.sem, dop.val))
                    nwait += 1
                    for k, v in dop.clock.items():
                        if kn.get(k, 0) < v:
                            kn[k] = v
            for d in sorted(o.deps):
                need(ops[d])
            if o.dma:
                k = dcnt[o.eng]
                dcnt[o.eng] += 1
                if k >= NPOOL:
                    need(dhist[o.eng][k - NPOOL])
                o.sem = dsems[o.eng][k % NPOOL]
                o.val = 16 * (k // NPOOL + 1)
                o.semkey = ("d", o.eng, k % NPOOL)
                o.clock = dict(kn)
                o.clock[o.semkey] = o.val
                dhist[o.eng].append(o)
                st.append(("o", o.fn, o.sem, 16))
            elif o.needs_inc:
                g = cnt[o.eng]
                cnt[o.eng] += 1
                o.sem = esems[o.eng][g // SEG]
                o.val = g % SEG + 1
                o.semkey = (o.eng, g // SEG)
                o.clock = dict(kn)
                o.clock[o.semkey] = o.val
                st.append(("o", o.fn, o.sem, 1))
            else:
                st.append(("o", o.fn, None, 0))
        self.stats = dict(nops=len(ops), nwait=nwait, ninc=dict(cnt), ndma=dict(dcnt))
        return streams


def replay(e, stream):
    for it in stream:
        if it[0] == "w":
            e.wait_ge(it[1], it[2])
        else:
            _, fn, sem, inc = it
            if fn is None:
                continue
            ins = fn(e)
            if sem is not None:
                ins.then_inc(sem, inc)


def make_consts(TP, PAST):
    T = TP + TS
    ident = np.eye(128, dtype=np.float32)
    s = np.arange(128)[:, None]
    t = np.arange(128)[None, :]
    dm = np.zeros((128, 5, 128), np.float32)
    dm[:, 0, :] = np.where(s > t, -1e30, 0.0)
    for h in range(4):
        slope = 2.0 ** (-2.0 * (h + 1))
        v = np.where(s > t, -2.0 * slope * (s - t), 0.0)
        v = np.where((s // 64) > (t // 64), -1e30, v)
        dm[:, 1 + h, :] = v
    U = (s <= t).astype(np.float32)
    A = np.zeros((32, 16, 64), np.float32)
    for p in range(4):
        for hh in range(2):
            off = 32 * hh
            h = 2 * p + hh
            aq = A[:, 0 * 8 + 0 * 4 + p, :]
            ak = A[:, 0 * 8 + 1 * 4 + p, :]
            for k in range(3):
                aq[8 * k + h, off + k] = 1.0
                aq[24, off + 3 + k] = 1.0
                ak[24, off + k] = 1.0
                ak[8 * k + h, off + 3 + k] = -1.0
            slope = 2.0 ** (-2.0 * (p + 1))
            aq = A[:, 1 * 8 + 0 * 4 + p, :]
            ak = A[:, 1 * 8 + 1 * 4 + p, :]
            aq[25, off + 0] = -slope
            aq[26, off + 1] = -slope
            aq[24, off + 2] = 1.0
            aq[24, off + 3] = 1.0
            ak[24, off + 0] = 1.0
            ak[24, off + 1] = 1.0
            ak[25, off + 2] = slope
            ak[26, off + 3] = slope
    rq = np.zeros((3, T), np.float32)
    rq[0] = 1.0
    tp = np.arange(TP)
    rq[1, :TP] = 128 * (tp // 128)
    rq[2, :TP] = tp % 128
    rq[1, TP:] = 0.0
    rq[2, TP:] = np.arange(TS) - 15.0
    rk = np.zeros((3, PAST), np.float32)
    rk[0] = 1.0
    sp = np.arange(PAST)
    rk[1] = -128.0 * (PAST // 128 - sp // 128)
    rk[2] = sp % 128 - 15.0
    qb = np.zeros((32, 2, 128), np.float32)
    dms = np.zeros((16, 2, 128), np.float32)
    mask = np.zeros((128, 2, 512), np.float32)
    for h in range(8):
        for k in range(3):
            qb[8 * k + h, 0, 16 * h:16 * h + 16] = -1.0
        dms[:, 0, 16 * h:16 * h + 16] = dm[:16, 0, :16]
        mask[16 * h:16 * h + 16, 0, 64 * h:64 * h + 64] = 1.0
    for c in range(2):
        for h in range(4):
            slope = 2.0 ** (-2.0 * (h + 1))
            cb = 64 * c + 16 * h
            qb[25, 1, cb:cb + 16] = slope
            qb[26, 1, cb:cb + 16] = slope
            dms[:, 1, cb:cb + 16] = dm[:16, 1 + h, :16]
            mask[cb:cb + 16, 1, 128 * h:128 * h + 128] = 1.0
    return dict(c_ident=ident, c_dm=dm.reshape(128, 640), c_U=U, c_A=A.reshape(32, 1024),
                c_rq=rq.astype(ml_dtypes.bfloat16), c_rk=rk.astype(ml_dtypes.bfloat16),
                c_qbias=qb.reshape(32, 256).astype(ml_dtypes.bfloat16), c_dms=dms.reshape(16, 256).astype(ml_dtypes.bfloat16),
                c_mask=mask.reshape(128, 1024).astype(ml_dtypes.bfloat16))


def build(cfg):
    TP, PAST, DEPTH = cfg["TP"], cfg["PAST"], cfg["DEPTH"]
    dbg_names = cfg.get("dbg", ())
    T = TP + TS
    NTP = TP // 128
    NT = NTP + 1
    tiles = [(i * 128, 128) for i in range(NTP)] + [(TP, TS)]
    NGP = TP // 512
    groups = [(g * 512, 512) for g in range(NGP)] + [(TP, TS)]
    NG = NGP + 1
    NPT = PAST // 128
    QT = min(8, NPT)
    NQ = NPT // QT
    TB = T * 2

    nc = bass.Bass("TRN2", target_bir_lowering=False)

    def din(name, shape):
        return nc.dram_tensor(name, list(shape), F32, kind="ExternalInput").ap()

    def dout(name, shape):
        return nc.dram_tensor(name, list(shape), F32, kind="ExternalOutput").ap()

    x0 = din("x0", [T, D])
    p_in = din("p", [DEPTH, T, PLE])
    ckf = din("ckf", [DEPTH, PAST, 512])
    cvf = din("cvf", [DEPTH, PAST, 512])
    clf = din("clf", [DEPTH, PAST, 8])
    ckd = din("ckd", [DEPTH, PAST, 512])
    cvd = din("cvd", [DEPTH, PAST, 512])
    w_in = din("w_in", [DEPTH, D, INC])
    b_f = din("b_forget", [DEPTH, 8])
    lq1 = din("lambda_q1", [1, DEPTH * 64])
    lk1 = din("lambda_k1", [1, DEPTH * 64])
    lq2 = din("lambda_q2", [1, DEPTH * 64])
    lk2 = din("lambda_k2", [1, DEPTH * 64])
    g_diff = din("diff_norm_g", [DEPTH, 512])
    w_ba = din("w_branch_fox", [DEPTH, 512, D])
    w_bb = din("w_branch_diff", [DEPTH, 512, D])
    w_o = din("w_out", [DEPTH, D, D])
    ln1_g = din("ln1_g", [DEPTH, D])
    ln1_b = din("ln1_b", [DEPTH, D])
    w_g = din("w_ffn_gate", [DEPTH, D, DFF])
    w_u = din("w_ffn_up", [DEPTH, D, DFF])
    w_d = din("w_ffn_down", [DEPTH, DFF, D])
    w_pg = din("w_ple_gate", [DEPTH, D, D])
    w_pp = din("w_ple_proj", [DEPTH, PLE, D])
    ln2_g = din("ln2_g", [DEPTH, D])
    ln2_b = din("ln2_b", [DEPTH, D])
    c_ident = din("c_ident", [128, 128])
    c_dm = din("c_dm", [128, 640])
    c_U = din("c_U", [128, 128])
    c_A = din("c_A", [32, 1024])
    c_rq = nc.dram_tensor("c_rq", [3, T], BF16, kind="ExternalInput").ap()
    c_rk = nc.dram_tensor("c_rk", [3, PAST], BF16, kind="ExternalInput").ap()
    c_qbias = nc.dram_tensor("c_qbias", [32, 256], BF16, kind="ExternalInput").ap()
    c_dms = nc.dram_tensor("c_dms", [16, 256], BF16, kind="ExternalInput").ap()
    c_mask = nc.dram_tensor("c_mask", [128, 1024], BF16, kind="ExternalInput").ap()

    y_out = dout("y", [T, D])
    o_fk = dout("o_fk", [DEPTH, T, 512])
    o_fv = dout("o_fv", [DEPTH, T, 512])
    o_lf = dout("o_lf", [DEPTH, T, 8])
    o_dk = dout("o_dk", [DEPTH, T, 512])
    o_dv = dout("o_dv", [DEPTH, T, 512])
    xres = nc.dram_tensor("xres", [T, D], F32).ap()

    P = Prog()
    out_dmas = []

    b_xT = Buf(nc, "xT", NCH * TB)
    xT = View(b_xT, 0, (NCH, T), BF16)
    SLOTB = 8192
    slots = [Buf(nc, f"slot{k}", SLOTB) for k in range(4)]
    b_c = Buf(nc, "consts", 256 + 1280 + 512 + 512 + 256 + 2048 + 256)
    ident = View(b_c, 0, (128,), BF16)
    dm = View(b_c, 256, (5, 128), BF16)
    Utri = View(b_c, 1536, (128,), F32)
    ones32 = View(b_c, 2048, (128,), F32)
    onesdiv = View(b_c, 2560, (128,), BF16)
    Amat = View(b_c, 2816, (16, 64), BF16)
    onesb = View(b_c, 4864, (128,), BF16)
    NLS = (NPT + 1) * 8
    smb = 0

    def salloc(n):
        nonlocal smb
        o = smb
        smb += (n + 31) // 32 * 32
        return o
    offs = {}
    for nm, n in [("lam", DEPTH * 4), ("neglam", DEPTH * 4), ("bfbc", DEPTH * 32), ("gsc", DEPTH * 16),
                  ("ls1", DEPTH * 4), ("ls2", DEPTH * 4),
                  ("Zf", NT * 32), ("LFa", NT * 32), ("LFe", NT * 32), ("LFl", NT * 32), ("LF", NT * 32),
                  ("Ls", NLS * 4), ("Zex", NLS * 4), ("Chi", NLS * 2), ("R1", NLS * 4), ("R2", NLS * 4),
                  ("TSq", NT * 48), ("TSk", NPT * 48), ("st6", 192), ("mv", 32), ("rstd", 16), ("Wf", 128)]:
        offs[nm] = salloc(n)
    b_s = Buf(nc, "small", smb)

    def sv(nm, shape, dt):
        return View(b_s, offs[nm], shape, dt)
    lam = sv("lam", (DEPTH,), F32)
    neglam = sv("neglam", (DEPTH,), F32)
    bfbc = sv("bfbc", (DEPTH * 8,), F32)
    gsc = sv("gsc", (DEPTH * 4,), F32)
    ls1 = sv("ls1", (DEPTH,), F32)
    ls2 = sv("ls2", (DEPTH,), F32)
    Zf = sv("Zf", (NT, 8), F32)
    LFa = sv("LFa", (NT, 8), F32)
    LFe = sv("LFe", (NT, 8), F32)
    LFl = sv("LFl", (NT, 8), F32)
    LF = sv("LF", (NT, 8), F32)
    Ls = sv("Ls", (NPT + 1, 8), F32)
    Zex = sv("Zex", (NPT + 1, 8), F32)
    Chi = sv("Chi", (NPT + 1, 8), BF16)
    R1 = sv("R1", (NPT + 1, 8), F32)
    R2 = sv("R2", (NPT + 1, 8), F32)
    TSq = sv("TSq", (NT, 24), BF16)
    TSk = sv("TSk", (NPT, 24), BF16)
    st6 = sv("st6", (4, 2, 6), F32)
    mv = sv("mv", (4, 2), F32)
    rstd = sv("rstd", (4,), F32)
    Wf = sv("Wf", (8, 8), BF16)

    Z_SZ = NT * 4096
    AB0 = 13 * TB + QT * 256 + 3 * 1024 + 2 * 2048 + 3 * 2048
    S0 = max(Z_SZ, AB0)
    S0 = (S0 + 63) // 64 * 64
    S1 = max(NT * 2048, 8 * TB)
    S1 = (S1 + 63) // 64 * 64
    S2 = 26624
    b_ov = Buf(nc, "ov", S0 + S1 + S2)
    zacc = View(b_ov, 0, (NT, D), F32)
    oaT = View(b_ov, 0, (4, T), BF16)
    obT = View(b_ov, 4 * TB, (4, T), BF16)
    QKt = [View(b_ov, (8 + k) * TB, (T,), BF16) for k in range(4)]
    Rq = View(b_ov, 12 * TB, (T,), BF16)
    o = 13 * TB
    Rk = View(b_ov, o, (QT * 128,), BF16); o += QT * 256
    PTb = [View(b_ov, o + k * 1024, (512,), BF16) for k in range(3)]; o += 3072
    RECb = [View(b_ov, o + k * 2048, (512,), F32) for k in range(2)]; o += 4096
    OCb = [View(b_ov, o + k * 2048, (512,), F32) for k in range(3)]; o += 6144
    assert o <= S0
    Vst = View(b_ov, S0, (NT, 8, 128), BF16)
    mergedT = View(b_ov, S0, (8, T), BF16)
    hT = View(b_ov, S0, (4, T), BF16)
    o2 = S0 + S1
    ystage = [View(b_ov, o2 + k * 2048, (512,), F32) for k in range(4)]
    Krb = [View(b_ov, o2 + 4096 + k * 4096, (4, 512), BF16) for k in range(2)]
    Vrb = [View(b_ov, o2 + 12288 + k * 4096, (4, 512), BF16) for k in range(2)]
    VnewS = View(b_ov, o2 + 20480, (512,), BF16)
    QblkS = View(b_ov, o2 + 21504, (4, 128), BF16)
    KnewS = View(b_ov, o2 + 22528, (4, 16), BF16)
    SQb = View(b_ov, o2 + 25600, (512,), BF16)
    RSb = RECb[0]
    VstD = View(b_ov, S0, (NT, 4, 128), BF16)
    if 4 * TB >= 14912:
        b_ss, SB0 = b_ov, 8 * TB
    else:
        b_ss, SB0 = Buf(nc, "sscr", 14912), 0
    KTb = [View(b_ss, SB0 + k * 2048, (1024,), BF16) for k in range(4)]
    RkTb = [View(b_ss, SB0 + 8192 + k * 1024, (512,), BF16) for k in range(2)]
    PTn = [View(b_ss, SB0 + 10240 + k * 1024, (512,), BF16) for k in range(2)]
    Zm = View(b_ss, SB0 + 4096, (512,), F32)
    MaskS = View(b_ov, o2 + 25600, (512,), BF16)
    QbS = View(b_ss, SB0 + 12288, (128,), BF16)
    DmS = View(b_ss, SB0 + 12544, (128,), BF16)
    Zs = View(b_ss, SB0 + 12800, (128,), F32)
    Znb = View(b_ss, SB0 + 13312, (128,), BF16)
    Znb2 = View(b_ss, SB0 + 13568, (128,), BF16)
    SQs = View(b_ss, SB0 + 13824, (128,), F32)
    Gs = View(b_ss, SB0 + 14336, (128,), F32)
    rsS = View(b_ss, SB0 + 14848, (8,), F32)
    ssS = View(b_ss, SB0 + 14880, (8,), F32)
    assert (b_ss is not b_ov) or SB0 + 14912 <= 12 * TB
    xrow = [View(b_ov, o2 + k * 4096, (D,), F32) for k in range(2)]
    LNG = View(b_ov, o2 + 8192, (D,), F32)
    LNB = View(b_ov, o2 + 12288, (D,), F32)
    XB = View(b_ov, o2 + 16384, (D,), BF16)
    XBs = [XB, View(b_ov, o2 + 24576, (D,), BF16)]
    CS = [View(b_ov, o2 + 18432 + k * 2048, (512,), F32) for k in range(4)]
    ptok = [View(b_ov, o2 + 18432 + k * 512, (PLE,), BF16) for k in range(2)]
    pTt = [View(b_ov, o2 + 19456 + k * 512, (2, 128), BF16) for k in range(2)]
    SGe = [View(b_ov, o2 + 20480 + k * 2048, (512,), F32) for k in range(2)]
    PLe = [View(b_ov, o2 + 4096 + k * 2048, (512,), F32) for k in range(1)]
    SLe = [View(b_ov, o2 + k * 2048, (512,), F32) for k in range(2)]
    lqa = View(b_ov, o2, (DEPTH, 64), F32)
    lqb = View(b_ov, o2 + 4096, (DEPTH, 64), F32)

    PS = [nc.alloc_psum_tensor(f"ps{b}", [128, 512], F32) for b in range(8)]
    PSB = [t.bitcast(BF16) for t in PS]

    def psk(b):
        return ("ps", b)
    rot = {"all": 0, "st": 0, "acc": 0, "pt": 0, "ys": 0}

    def bank():
        allowed = {"all": (0, 1, 2, 3, 4, 5, 6, 7), "proj": (3, 4, 5, 6, 7), "sample": (7, 0, 1, 2)}[rot.get("mode", "all")]
        while True:
            rot["all"] = (rot["all"] + 1) % 8
            if rot["all"] in allowed:
                return rot["all"]
    ST_BANKS = (0, 1, 2)
    ACC_BANKS = (3, 4, 5, 6)

    def stbank():
        rot["st"] = (rot["st"] + 1) % 3
        return ST_BANKS[rot["st"]]

    def ptbuf():
        rot["pt"] = (rot["pt"] + 1) % 3
        return PTb[rot["pt"]]

    def pe_mm(mms, R, W):
        def fn(e, mms=mms):
            ins = None
            for (o_, l_, r_, s_, t_) in mms:
                ins = e.matmul(o_, lhsT=l_, rhs=r_, start=s_, stop=t_)
            return ins
        return P.op("pe", fn, R, W)

    def pe_tr(trs, R, W):
        def fn(e, trs=trs):
            ins = None
            for (o_, i_, id_) in trs:
                ins = e.transpose(o_, i_, id_)
            return ins
        return P.op("pe", fn, R, W)

    def dma(eng, out_ap, in_ap, R, W, is_out=False, nonc=False):
        def fn(e, o_=out_ap, i_=in_ap):
            if nonc:
                return e.dma_start(out=o_, in_=i_, allow_slow_non_contiguous=True)
            return e.dma_start(out=o_, in_=i_)
        k = P.op(eng, fn, R, W, dma=True)
        if is_out:
            out_dmas.append(k)
        return k

    def act(out, in_, func, R, W, **kw):
        return P.op("act", lambda e: e.activation(out=out, in_=in_, func=func, **kw), R, W)

    def copy(eng, out, in_, R, W):
        if eng == "act":
            return act(out, in_, AF.Copy, R, W)
        return P.op(eng, lambda e: e.tensor_copy(out=out, in_=in_), R, W)

    def tt(eng, out, in0, in1, op, R, W):
        return P.op(eng, lambda e: e.tensor_tensor(out=out, in0=in0, in1=in1, op=op), R, W)

    def tsc(eng, out, in0, s1, s2, op0, op1, R, W):
        if op1 is None:
            return P.op(eng, lambda e: e.tensor_scalar(out=out, in0=in0, scalar1=s1, scalar2=None, op0=op0), R, W)
        return P.op(eng, lambda e: e.tensor_scalar(out=out, in0=in0, scalar1=s1, scalar2=s2, op0=op0, op1=op1), R, W)

    def stt(eng, out, in0, scalar, in1, op0, op1, R, W):
        return P.op(eng, lambda e: e.scalar_tensor_tensor(out=out, in0=in0, scalar=scalar, in1=in1, op0=op0, op1=op1), R, W)

    def memset(eng, sub, val):
        return P.op(eng, lambda e: e.memset(sub.ap, val), (), [sub])

    dbg_outs = {}

    def dbg(name, sub, shape):
        if name not in dbg_names:
            return
        d = nc.dram_tensor("dbg_" + name, list(shape), sub.ap.dtype, kind="ExternalOutput").ap()
        dbg_outs[name] = d
        dma("sp", d, sub.ap, [sub], [("dbg", name)], is_out=True)

    dma("pool", ident[:].ap, c_ident[:, :], (), [ident[:]])
    dma("pool", dm[:].ap, c_dm.rearrange("p (a b) -> p a b", b=128), (), [dm[:]])
    dma("sp", Utri[:].ap, c_U[:, :], (), [Utri[:]])
    dma("pool", Amat[0:32].ap, c_A.rearrange("p (a b) -> p a b", b=64), (), [Amat[0:32]])
    memset("dve", ones32[:], 1.0)
    memset("dve", onesdiv[:], 1.0 / 128.0)
    memset("dve", onesb[:], 1.0)
    for v_ in (Zf, LF, Ls, Zex, TSq, TSk):
        memset("dve", v_[:], 0.0)
    dma("sp", bfbc[:].ap, b_f.rearrange("l h -> (l h)").rearrange("(o n) -> o n", o=1).to_broadcast([128, DEPTH * 8]), (), [bfbc[:]])
    dma("sp", lqa[:].ap, lq1.to_broadcast([128, DEPTH * 64]).rearrange("p (l e) -> p l e", e=64), (), [lqa[:]])
    dma("sp", lqb[:].ap, lk1.to_broadcast([128, DEPTH * 64]).rearrange("p (l e) -> p l e", e=64), (), [lqb[:]])
    tt("dve", lqa[:].ap, lqa[:].ap, lqb[:].ap, ALU.mult, [lqa[:], lqb[:]], [lqa[:]])
    P.op("dve", lambda e: e.tensor_reduce(out=ls1[:].ap, in_=lqa[:].ap, axis=AX.X, op=ALU.add), [lqa[:]], [ls1[:]])
    dma("sp", lqa[:].ap, lq2.to_broadcast([128, DEPTH * 64]).rearrange("p (l e) -> p l e", e=64), (), [lqa[:]])
    dma("sp", lqb[:].ap, lk2.to_broadcast([128, DEPTH * 64]).rearrange("p (l e) -> p l e", e=64), (), [lqb[:]])
    tt("dve", lqa[:].ap, lqa[:].ap, lqb[:].ap, ALU.mult, [lqa[:], lqb[:]], [lqa[:]])
    P.op("dve", lambda e: e.tensor_reduce(out=ls2[:].ap, in_=lqa[:].ap, axis=AX.X, op=ALU.add), [lqa[:]], [ls2[:]])
    act(ls1[:].ap, ls1[:].ap, AF.Exp, [ls1[:]], [ls1[:]])
    act(ls2[:].ap, ls2[:].ap, AF.Exp, [ls2[:]], [ls2[:]])
    tt("dve", lam[:].ap, ls1[:].ap, ls2[:].ap, ALU.subtract, [ls1[:], ls2[:]], [lam[:]])
    for l in range(DEPTH):
        tsc("dve", lam[:, l:l + 1].ap, lam[:, l:l + 1].ap, float(lambda_init(l)), None, ALU.add, None, [lam[:]], [lam[:]])
    tsc("dve", neglam[:].ap, lam[:].ap, -1.0, None, ALU.mult, None, [lam[:]], [neglam[:]])
    dma("sp", gsc[:].ap.rearrange("p (l h) -> p l h", h=4), g_diff.rearrange("l (h e) -> e l h", e=128), (), [gsc[:]], nonc=True)
    for l in range(DEPTH):
        tsc("dve", gsc[:, 4 * l:4 * l + 4].ap, gsc[:, 4 * l:4 * l + 4].ap, float(1.0 - lambda_init(l)), None, ALU.mult, None, [gsc[:]], [gsc[:]])

    slot_rr = [0]

    def wload(src2d, nch, col0, ncols, dst_view=None, c_off=0, col_off=0):
        v = dst_view
        for c in range(0, nch, 2):
            c1 = min(nch, c + 2)
            src = src2d[c * 128:c1 * 128, col0:col0 + ncols].rearrange("(c p) n -> p c n", p=128)
            dst = v[:, c_off + c:c_off + c1, col_off:col_off + ncols]
            dma("pool", dst.ap, src, (), [dst])

    def new_slot(shape):
        k = slot_rr[0]
        slot_rr[0] = (k + 1) % 4
        assert int(np.prod(shape)) * 2 <= SLOTB, shape
        return View(slots[k], 0, shape, BF16)

    def make_xT(i, src_bf, eng="dve"):
        r0, n = tiles[i]
        b = bank()
        trs = [(PSB[b][:, c * 128:c * 128 + n], src_bf.ap[0:n, c * 128:(c + 1) * 128], ident[0:n, 0:n].ap) for c in range(NCH)]
        pe_tr(trs, [src_bf, ident[:]], [psk(b)])
        src = PSB[b][:, 0:1024].rearrange("p (c t) -> p c t", t=128)[:, :, 0:n]
        dst = xT[:, :, r0:r0 + n]
        copy(eng, dst.ap, src, [], [psk(b), dst])

    def ln_stage1(i):
        r0, n = tiles[i]
        k2 = i % 4
        for k in range(2):
            zz = zacc[0:n, i, k * 512:(k + 1) * 512]
            P.op("dve", lambda e, zz=zz, k=k: e.bn_stats(out=st6[0:n, k2, k, :].ap, in_=zz.ap), [zz], [st6[:, k2, k, :]])
        P.op("dve", lambda e: e.bn_aggr(out=mv[0:n, k2, :].ap, in_=st6[0:n, k2, :, :].ap.rearrange("p a b -> p (a b)")), [st6[:, k2]], [mv[:, k2]])
        act(rstd[0:n, k2:k2 + 1].ap, mv[0:n, k2, 1:2].ap, AF.Ln, [mv[:, k2]], [rstd[:, k2:k2 + 1]], bias=LN_EPS)
        act(rstd[0:n, k2:k2 + 1].ap, rstd[0:n, k2:k2 + 1].ap, AF.Exp, [rstd[:, k2:k2 + 1]], [rstd[:, k2:k2 + 1]], scale=-0.5)

    def ln_stage2(i, gv, bv, l, final_out, make_t):
        r0, n = tiles[i]
        k2 = i % 4
        z = zacc[0:n, i, :]
        stt("dve", z.ap, z.ap, mv[0:n, k2, 0:1].ap, gv[0:n, :].ap, ALU.subtract, ALU.mult, [z, mv[:, k2], gv[:]], [z])
        stt("dve", z.ap, z.ap, rstd[0:n, k2:k2 + 1].ap, bv[0:n, :].ap, ALU.mult, ALU.add, [z, rstd[:, k2:k2 + 1], bv[:]], [z])
        if final_out:
            dma("sp", y_out[r0:r0 + n, :], z.ap, [z], [("y", i)], is_out=True)
        else:
            dma("sp", xres[r0:r0 + n, :], z.ap, [z], [("xres", i)])
        if make_t:
            xb = XBs[i % 2][0:n, :]
            copy("act", xb.ap, z.ap, [z], [xb])
            make_xT(i, xb, eng="act")

    for i, (r0, n) in enumerate(tiles):
        xb = XB[0:n, :]
        dma("pool", xb.ap, x0[r0:r0 + n, :], (), [xb])
        make_xT(i, xb)

    def phase_A(l, br, wk, wv):
        okd, ovd = (o_fk, o_fv) if br == 0 else (o_dk, o_dv)
        if br == 0:
            memset("pool", Vst[:, :, :, 64:128], 1.0)
        for i, (r0, n) in enumerate(tiles):
            for part, (wv_, od) in enumerate(((wk, okd), (wv, ovd))):
                b = bank()
                mms = [(PS[b][0:n, :], xT[:, c, r0:r0 + n].ap, wv_[:, c, :].ap, c == 0, c == NCH - 1) for c in range(NCH)]
                pe_mm(mms, [xT[:, :, r0:r0 + n], wv_[:]], [psk(b)])
                rot["ys"] = (rot["ys"] + 1) % 4
                ys = ystage[rot["ys"]][0:n, :]
                if part == 0:
                    copy("act", ys.ap, PS[b][0:n, :], [], [psk(b), ys])
                else:
                    copy("dve", ys.ap, PS[b][0:n, :], [], [psk(b), ys])
                    if br == 0:
                        vd = Vst[0:n, i, :, 0:64]
                        copy("pool", vd.ap, ys.ap.rearrange("p (g e) -> p g e", e=64), [ys], [vd])
                    else:
                        vd = VstD[0:n, i, :, :]
                        copy("pool", vd.ap, ys.ap.rearrange("p (g e) -> p g e", e=128), [ys], [vd])
                    if i == NTP:
                        copy("dve", VnewS[0:n, :].ap, ys.ap, [ys], [VnewS[0:n, :]])
                dma("sp", od[l, r0:r0 + n, :], ys.ap, [ys], [("okv", br, part, i)], is_out=True)

    f_chain_state = {}

    def f_chain(l):
        dma("pool", Wf[:].ap, w_in[l][:, 1536:1544].rearrange("(c p) n -> p c n", p=128), (), [Wf[:]], nonc=True)
        for i, (r0, n) in enumerate(tiles):
            b = bank()
            mms = [(PS[b][0:n, 0:8], xT[:, c, r0:r0 + n].ap, Wf[:, c, :].ap, c == 0, c == NCH - 1) for c in range(NCH)]
            pe_mm(mms, [xT[:, :, r0:r0 + n], Wf[:]], [psk(b)])
            zf = Zf[0:n, i, :]
            tt("dve", zf.ap, PS[b][0:n, 0:8], bfbc[0:n, 8 * l:8 * l + 8].ap, ALU.add, [bfbc[:]], [psk(b), zf])
        tsc("dve", LFa[:].ap, Zf[:].ap, -1.0, None, ALU.mult, None, [Zf[:]], [LFa[:]])
        tt("dve", LFa[:].ap, LFa[:].ap, Zf[:].ap, ALU.min, [LFa[:], Zf[:]], [LFa[:]])
        act(LFe[:].ap, LFa[:].ap, AF.Exp, [LFa[:]], [LFe[:]])
        act(LFl[:].ap, LFe[:].ap, AF.Ln, [LFe[:]], [LFl[:]], bias=1.0)
        tsc("dve", LFa[:].ap, Zf[:].ap, 0.0, None, ALU.min, None, [Zf[:]], [LFa[:]])
        tt("dve", LF[:].ap, LFa[:].ap, LFl[:].ap, ALU.subtract, [LFa[:], LFl[:]], [LF[:]])
        dma("sp", o_lf[l, 0:TP, :].rearrange("(i p) h -> p i h", p=128), LF[:, 0:NTP, :].ap, [LF[:]], [("olf", 0)], is_out=True, nonc=True)
        dma("sp", o_lf[l, TP:T, :], LF[0:TS, NTP, :].ap, [LF[:]], [("olf", 1)], is_out=True, nonc=True)

        def cumsum(Lv, ntile, dsts, rel=False, Zex=Zex):
            b = bank()
            N = ntile * 8
            mms = [(PS[b][:, 0:N], Utri[:].ap, Lv[:, 0:ntile, :].ap.rearrange("p a b -> p (a b)"), True, False),
                   (PS[b][:, 0:N], ones32[:].ap, Zex[:, 0:ntile, :].ap.rearrange("p a b -> p (a b)"), False, not rel)]
            Rr = [Utri[:], ones32[:], Lv[:, 0:ntile, :], Zex[:, 0:ntile, :]]
            if rel:
                tt("dve", R1[:, 0, :].ap, Zex[:, ntile - 1, :].ap, Lv[:, ntile - 1, :].ap, ALU.add,
                   [Zex[:, ntile - 1, :], Lv[:, ntile - 1, :]], [R1[:, 0, :]])
                for j in range(ntile):
                    tsc("dve", R2[:, j, :].ap, R1[:, 0, :].ap, -1.0, None, ALU.mult, None, [R1[:, 0, :]], [R2[:, j, :]])
                mms.append((PS[b][:, 0:N], ones32[:].ap, R2[:, 0:ntile, :].ap.rearrange("p a b -> p (a b)"), False, True))
                Rr.append(R2[:, 0:ntile, :])
            pe_mm(mms, Rr, [psk(b)])
            cps = PS[b][:, 0:N].rearrange("p (a b) -> p a b", b=8)
            ch = Chi[:, 0:ntile, :]
            r1 = R1[:, 0:ntile, :]
            r2 = R2[:, 0:ntile, :]
            copy("dve", ch.ap, cps, [], [psk(b), ch])
            tt("dve", r1.ap, cps, ch.ap, ALU.subtract, [ch], [psk(b), r1])
            for (j0, j1, tsv, toff) in dsts:
                d0 = tsv[:, j0 - toff:j1 - toff, 0:8]
                copy("dve", d0.ap, Chi[:, j0:j1, :].ap, [Chi[:, j0:j1, :]], [d0])
            ch2 = Chi[:, 0:ntile, :]
            copy("dve", ch2.ap, r1.ap, [r1], [ch2])
            tt("dve", r2.ap, r1.ap, ch2.ap, ALU.subtract, [r1, ch2], [r2])
            for (j0, j1, tsv, toff) in dsts:
                d1 = tsv[:, j0 - toff:j1 - toff, 8:16]
                copy("dve", d1.ap, Chi[:, j0:j1, :].ap, [Chi[:, j0:j1, :]], [d1])
                d2 = tsv[:, j0 - toff:j1 - toff, 16:24]
                copy("dve", d2.ap, R2[:, j0:j1, :].ap, [R2[:, j0:j1, :]], [d2])

        def prefix(Lv, ntile, zex):
            memset("pool", zex[:, 0, :], 0.0)
            for j in range(1, ntile):
                tt("pool", zex[:, j, :].ap, zex[:, j - 1, :].ap, Lv[:, j - 1, :].ap, ALU.add, [zex[:, j - 1, :], Lv[:, j - 1, :]], [zex[:, j, :]])
        prefix(LF, NTP, R1)
        dma("sp", Ls[:, 0:NPT, :].ap, clf[l].rearrange("(j p) h -> p j h", p=128), (), [Ls[:, 0:NPT, :]], nonc=True)
        copy("dve", Ls[0:TS, NPT, :].ap, LF[0:TS, NTP, :].ap, [LF[:]], [Ls[:, NPT, :]])
        prefix(Ls, NPT + 1, Zex)
        f_chain_state["cumsum"] = cumsum

    def f_chain_b(l):
        cumsum = f_chain_state["cumsum"]
        cumsum(LF, NTP, [(0, NTP, TSq, 0)], Zex=R1)
        cumsum(Ls, NPT + 1, [(0, NPT, TSk, 0), (NPT, NPT + 1, TSq, NPT - NTP)], rel=True)
        memset("pool", Rq[0:32, :], 0.0)
        dma("sp", Rq[24:27, :].ap, c_rq[:, :], (), [Rq[0:32, :]])
        for i0 in range(0, NT, 4):
            b = bank()
            trs = []
            cols = 0
            for i in range(i0, min(NT, i0 + 4)):
                r0, n = tiles[i]
                trs.append((PSB[b][0:24, (i - i0) * 128:(i - i0) * 128 + n], TSq[0:n, i, :].ap, ident[0:n, 0:n].ap))
                cols = (i - i0) * 128 + n
            pe_tr(trs, [TSq[:], ident[:]], [psk(b)])
            c0 = tiles[i0][0]
            dst = Rq[0:24, c0:c0 + cols]
            copy("dve", dst.ap, PSB[b][0:24, 0:cols], [], [psk(b), dst])

    def rq_consts_only():
        memset("pool", Rq[0:32, :], 0.0)
        dma("sp", Rq[24:27, :].ap, c_rq[:, :], (), [Rq[0:32, :]])

    def normalize_fox(acc, w, dst):
        rec = RECb[0][64:128, 0:w]
        P.op("dve", lambda e: e.reciprocal(out=rec.ap, in_=PS[acc][64:128, 0:w]), [], [psk(acc), rec])
        tt("dve", dst.ap, PS[acc][0:64, 0:w], rec.ap, ALU.mult, [rec], [psk(acc), dst])

    def sample_prefetch(l, br, bi, what="kv"):
        ck, cv = (ckf, cvf) if br == 0 else (ckd, cvd)
        kb, vb = Krb[bi % 2], Vrb[bi % 2]
        if "k" in what:
            dma("pool", kb[:].ap, ck[l, bi * 512:(bi + 1) * 512, :].rearrange("(j p) c -> p j c", p=128), (), [kb[:]])
        if "v" in what:
            dma("pool", vb[:].ap, cv[l, bi * 512:(bi + 1) * 512, :].rearrange("(j p) c -> p j c", p=128), (), [vb[:]])

    def sample_batched(l, br):
        NB = NPT // 4
        rot["mode"] = "sample"
        OUTB, SUMB = 3, 4
        dma("sp", QbS[0:32, :].ap, c_qbias[:, br * 128:(br + 1) * 128], (), [QbS[0:32, :]])
        dma("sp", DmS[0:TS, :].ap, c_dms[:, br * 128:(br + 1) * 128], (), [DmS[0:TS, :]])
        dma("sp", MaskS[:].ap, c_mask[:, br * 512:(br + 1) * 512], (), [MaskS[:]])
        if br == 1:
            for h in range(4):
                dma("sp", Gs[16 * h:16 * h + 16, :].ap, g_diff[l:l + 1, h * 128:(h + 1) * 128].to_broadcast([16, 128]), (), [Gs[0:64, :]])
            tsc("dve", Gs[0:64, :].ap, Gs[0:64, :].ap, float(1.0 - lambda_init(l)), None, ALU.mult, None, [Gs[0:64, :]], [Gs[0:64, :]])
        for k in range(2):
            memset("pool", RkTb[k][0:32, :], 0.0)

        def stage_a(bi):
            kb, rk = Krb[bi % 2], RkTb[bi % 2]
            dma("sp", rk[24:27, :].ap, c_rk[:, bi * 512:(bi + 1) * 512], (), [rk[0:32, :]])
            if br == 0:
                b = bank()
                pe_tr([(PSB[b][0:24, j * 128:(j + 1) * 128], TSk[:, bi * 4 + j, :].ap, ident[:].ap) for j in range(4)], [TSk[:], ident[:]], [psk(b)])
                copy("dve", rk[0:24, :].ap, PSB[b][0:24, 0:512], [], [psk(b), rk[0:24, :]])
            for half in range(2):
                b = bank()
                kt = KTb[2 * (bi % 2) + half]
                trs = []
                for jl in range(2):
                    for p in range(4):
                        q_ = jl * 4 + p
                        trs.append((PSB[b][:, q_ * 128:(q_ + 1) * 128], kb[:, 2 * half + jl, p * 128:(p + 1) * 128].ap, ident[:].ap))
                pe_tr(trs, [kb[:, 2 * half:2 * half + 2, :], ident[:]], [psk(b)])
                copy("act" if half else "dve", kt[:].ap, PSB[b][:, 0:1024], [], [psk(b), kt[:]])
            if bi + 2 < NB:
                sample_prefetch(l, br, bi + 2, "k")

        def stage_b(bi):
            rk = RkTb[bi % 2]
            st = stbank()
            for jl in range(4):
                kt = KTb[2 * (bi % 2) + jl // 2]
                mms = []
                for p in range(4):
                    q_ = (jl % 2) * 4 + p
                    mms.append((PS[st][:, jl * 128:(jl + 1) * 128], kt[:, q_ * 128:(q_ + 1) * 128].ap, QblkS[:, p, :].ap, p == 0, False))
                mms.append((PS[st][:, jl * 128:(jl + 1) * 128], rk[0:32, jl * 128:(jl + 1) * 128].ap, QbS[0:32, :].ap, False, True))
                pe_mm(mms, [kt[:], QblkS[:], rk[0:32, :], QbS[0:32, :]], [psk(st)])
            pt = PTn[bi % 2]
            act(pt[:].ap, PS[st][:, :], AF.Exp, [], [psk(st), pt[:]])

        def stage_c(bi):
            vb, pt = Vrb[bi % 2], PTn[bi % 2]
            pe_mm([(PS[OUTB][:, :], pt[:, jl * 128:(jl + 1) * 128].ap, vb[:, jl, :].ap, bi == 0 and jl == 0, False) for jl in range(4)],
                  [pt[:], vb[:]], [psk(OUTB)])
            pe_mm([(PS[SUMB][:, 0:8], pt[:, jl * 128:(jl + 1) * 128].ap, onesb[:, 0:8].ap, bi == 0 and jl == 0, False) for jl in range(4)],
                  [pt[:], onesb[:]], [psk(SUMB)])
            if bi + 2 < NB:
                sample_prefetch(l, br, bi + 2, "v")

        stage_a(0)
        for bi in range(NB):
            if bi + 1 < NB:
                stage_a(bi + 1)
            stage_b(bi)
            if bi >= 1:
                stage_c(bi - 1)
        stage_c(NB - 1)
        st = stbank()
        mms = [(PS[st][0:TS, 0:128], KnewS[:, p, :].ap, QblkS[:, p, :].ap, p == 0, False) for p in range(4)]
        mms.append((PS[st][0:TS, 0:128], Rq[0:32, TP:T].ap, QbS[0:32, :].ap, False, False))
        mms.append((PS[st][0:TS, 0:128], ident[0:TS, 0:TS].ap, DmS[0:TS, :].ap, False, True))
        pe_mm(mms, [KnewS[:], QblkS[:], Rq[0:32, TP:T], QbS[0:32, :], ident[:], DmS[0:TS, :]], [psk(st)])
        ptn = PTn[NB % 2][0:TS, 0:128]
        act(ptn.ap, PS[st][0:TS, 0:128], AF.Exp, [], [psk(st), ptn])
        pe_mm([(PS[OUTB][:, :], ptn.ap, VnewS[0:TS, :].ap, False, True)], [ptn, VnewS[0:TS, :]], [psk(OUTB)])
        pe_mm([(PS[SUMB][:, 0:8], ptn.ap, onesb[0:TS, 0:8].ap, False, True)], [ptn, onesb[:]], [psk(SUMB)])
        P.op("dve", lambda e: e.reciprocal(out=rsS[:, 0:1].ap, in_=PS[SUMB][:, 0:1]), [], [psk(SUMB), rsS[:]])
        tt("dve", Zm[:].ap, PS[OUTB][:, :], MaskS[:].ap, ALU.mult, [MaskS[:]], [psk(OUTB), Zm[:]])
        P.op("dve", lambda e: e.tensor_reduce(out=Zs[:].ap, in_=Zm[:].ap.rearrange("p (a x) -> p x a", a=4), axis=AX.X, op=ALU.add), [Zm[:]], [Zs[:]])
        if br == 0:
            b = bank()
            tsc("dve", Znb[:].ap, Zs[:].ap, rsS[:, 0:1].ap, None, ALU.mult, None, [Zs[:], rsS[:]], [Znb[:]])
            pe_tr([(PSB[b][:, 0:128], Znb[:].ap, ident[:].ap)], [Znb[:], ident[:]], [psk(b)])
            for hl in range(2):
                dst = oaT[64 * hl:64 * hl + 64, :, TP:T]
                src = PSB[b][64 * hl:64 * hl + 64, 0:128].rearrange("q (p hl t) -> q p hl t", hl=2, t=TS)[:, :, hl, :]
                copy("dve", dst.ap, src, [], [psk(b), dst])
        else:
            tsc("dve", Znb[0:64, :].ap, Zs[0:64, :].ap, rsS[0:64, 0:1].ap, None, ALU.mult, None, [Zs[:], rsS[:]], [Znb[0:64, :]])
            tsc("dve", Znb[64:128, :].ap, Zs[64:128, :].ap, rsS[64:128, 0:1].ap, neglam[64:128, l:l + 1].ap, ALU.mult, ALU.mult,
                [Zs[:], rsS[:], neglam[:]], [Znb[64:128, :]])
            b = bank()
            pe_mm([(PS[b][0:64, 0:128], ident[:, 0:64].ap, Znb[:].ap, True, False),
                   (PS[b][0:64, 0:128], ident[:, 64:128].ap, Znb[:].ap, False, True)], [ident[:], Znb[:]], [psk(b)])
            act(SQs[0:64, :].ap, PS[b][0:64, 0:128], AF.Square, [], [psk(b), SQs[0:64, :], ssS[0:64, 0:1]], accum_out=ssS[0:64, 0:1].ap)
            act(ssS[0:64, 1:2].ap, ssS[0:64, 0:1].ap, AF.Ln, [ssS[0:64, 0:1]], [ssS[0:64, 1:2]], scale=1.0 / 128.0, bias=RMS_EPS)
            act(ssS[0:64, 1:2].ap, ssS[0:64, 1:2].ap, AF.Exp, [ssS[0:64, 1:2]], [ssS[0:64, 1:2]], scale=-0.5)
            stt("dve", Znb2[0:64, :].ap, PS[b][0:64, 0:128], ssS[0:64, 1:2].ap, Gs[0:64, :].ap, ALU.mult, ALU.mult,
                [ssS[0:64, 1:2], Gs[0:64, :]], [psk(b), Znb2[0:64, :]])
            b2 = bank()
            pe_tr([(PSB[b2][:, 0:64], Znb2[0:64, :].ap, ident[0:64, 0:64].ap)], [Znb2[0:64, :], ident[:]], [psk(b2)])
            dst = obT[:, :, TP:T]
            copy("dve", dst.ap, PSB[b2][:, 0:64].rearrange("q (h t) -> q h t", t=TS), [], [psk(b2), dst])
        rot["mode"] = "proj"

    def phase_B(l, br, wq, wk):
        ck, cv = (ckf, cvf) if br == 0 else (ckd, cvd)
        QA, QB, KA, KB = QKt
        rot["mode"] = "proj"
        memset("pool", QblkS[:], 0.0)
        for bi in range(min(2, NPT // 4)):
            sample_prefetch(l, br, bi)
        for p in range(4):
            Aq = Amat[0:32, br * 8 + p, :]
            Ak = Amat[0:32, br * 8 + 4 + p, :]
            for g, (c0, w) in enumerate(groups):
                xs = xT[:, :, c0:c0 + w]
                bq = bank()
                pe_mm([(PS[bq][:, 0:w], wq[:, c, p * 128:(p + 1) * 128].ap, xT[:, c, c0:c0 + w].ap, c == 0, c == NCH - 1) for c in range(NCH)],
                      [xs, wq[:]], [psk(bq)])
                act(QA[0:64, c0:c0 + w].ap, PS[bq][0:64, 0:w], AF.Copy, [], [psk(bq), QA[0:64, c0:c0 + w]], scale=0.125)
                act(QB[0:64, c0:c0 + w].ap, PS[bq][64:128, 0:w], AF.Copy, [], [psk(bq), QB[0:64, c0:c0 + w]], scale=0.125)
                bk = bank()
                pe_mm([(PS[bk][:, 0:w], wk[:, c, p * 128:(p + 1) * 128].ap, xT[:, c, c0:c0 + w].ap, c == 0, c == NCH - 1) for c in range(NCH)],
                      [xs, wk[:]], [psk(bk)])
                copy("dve", KA[0:64, c0:c0 + w].ap, PS[bk][0:64, 0:w], [], [psk(bk), KA[0:64, c0:c0 + w]])
                copy("dve", KB[0:64, c0:c0 + w].ap, PS[bk][64:128, 0:w], [], [psk(bk), KB[0:64, c0:c0 + w]])
                ba = bank()
                rqs = Rq[0:32, c0:c0 + w]
                pe_mm([(PS[ba][0:64, 0:w], Aq.ap, rqs.ap, True, True), (PS[ba][64:128, 0:w], Ak.ap, rqs.ap, True, True)],
                      [Amat[0:32], rqs], [psk(ba)])
                copy("act", QA[64:96, c0:c0 + w].ap, PS[ba][0:32, 0:w], [], [psk(ba), QA[64:96, c0:c0 + w]])
                copy("act", QB[64:96, c0:c0 + w].ap, PS[ba][32:64, 0:w], [], [psk(ba), QB[64:96, c0:c0 + w]])
                copy("dve", KA[64:96, c0:c0 + w].ap, PS[ba][64:96, 0:w], [], [psk(ba), KA[64:96, c0:c0 + w]])
                copy("dve", KB[64:96, c0:c0 + w].ap, PS[ba][96:128, 0:w], [], [psk(ba), KB[64:96, c0:c0 + w]])
            if br == 0 and "qa" in dbg_names and p == 0 and l == 0:
                dbg("qa", QA[0:96, :], [96, T])
                dbg("ka", KA[0:96, :], [96, T])
            for hh, (Qh_, Kh_) in enumerate(((QA, KA), (QB, KB))):
                cb = (2 * p + hh) * 16 if br == 0 else hh * 64 + p * 16
                dq = QblkS[64 * hh:64 * hh + 64, p, cb:cb + 16]
                copy("dve", dq.ap, Qh_[0:64, TP:T].ap, [Qh_[0:64, TP:T]], [dq])
                dk = KnewS[64 * hh:64 * hh + 64, p, :]
                copy("dve", dk.ap, Kh_[0:64, TP:T].ap, [Kh_[0:64, TP:T]], [dk])
            heads = ((QA, KA), (QB, KB))
            dmi = 0 if br == 0 else 1 + p
            nacc = 1 if br == 0 else 2

            def vlhs(hh, a, tile_j, rows=128, p=p):
                if br == 0:
                    return Vst[0:rows, tile_j, 2 * p + hh, :]
                return VstD[0:rows, tile_j, p, :] if a == 0 else onesb[0:rows, :]
            pending_fin = []
            for m in range(NGP):
                jmax = 4 * m + 3
                accs_h = []
                for hh, (Qh, Kh) in enumerate(heads):
                    accs = []
                    for a in range(nacc):
                        rot["acc"] = (rot["acc"] + 1) % 4
                        accs.append(ACC_BANKS[rot["acc"]])
                    accs_h.append(accs)
                    pend = []
                    for j in range(jmax + 3):
                        cur = None
                        if j <= jmax:
                            cs = max(0, 128 * (j - 4 * m))
                            w = 512 - cs
                            st = stbank()
                            ks = Kh[0:96, 128 * j:128 * j + 128]
                            qs = Qh[0:96, 512 * m + cs:512 * m + 512]
                            mms = [(PS[st][:, cs:512], ks.ap, qs.ap, True, j < 4 * m)]
                            R = [ks, qs]
                            if j >= 4 * m:
                                mms.append((PS[st][:, cs:cs + 128], ident[:].ap, dm[:, dmi, :].ap, False, True))
                                R += [ident[:], dm[:]]
                            pe_mm(mms, R, [psk(st)])
                            pt = ptbuf()
                            pts = pt[:, cs:512]
                            act(pts.ap, PS[st][:, cs:512], AF.Exp, [], [psk(st), pts])
                            cur = (j, cs, pts)
                        if cur is not None:
                            pend.append(cur)
                        if pend and (len(pend) > 2 or cur is None):
                            jj, cs2, pts2 = pend.pop(0)
                            for a in range(nacc):
                                vs = vlhs(hh, a, jj)
                                pe_mm([(PS[accs[a]][:, cs2:512], vs.ap, pts2.ap, jj == 0, jj == jmax)], [vs, pts2], [psk(accs[a])])
                    c0 = 512 * m
                    if pending_fin:
                        pending_fin.pop(0)()
                    if br == 0:
                        normalize_fox(accs[0], 512, oaT[64 * hh:64 * hh + 64, p, c0:c0 + 512])
                    else:
                        diff_norm(hh, accs, 512)
                if br == 1:
                    pending_fin.append(lambda l=l, p=p, m=m: diff_finish(l, p, 512 * m, 512))
            while pending_fin:
                pending_fin.pop(0)()

        sample_batched(l, br)

    def diff_norm(c, accs, w):
        rec = RECb[c][:, 0:w]
        if c == 0:
            P.op("dve", lambda e: e.reciprocal(out=rec.ap, in_=PS[accs[1]][:, 0:w]), [], [psk(accs[1]), rec])
        else:
            act(rec.ap, PS[accs[1]][:, 0:w], AF.Ln, [], [psk(accs[1]), rec])
            act(rec.ap, rec.ap, AF.Exp, [rec], [rec], scale=-1.0)
        oc = OCb[c][:, 0:w]
        tt("dve", oc.ap, PS[accs[0]][:, 0:w], rec.ap, ALU.mult, [rec], [psk(accs[0]), oc])

    def diff_finish(l, h, c0, w):
        ob = OCb[2][:, 0:w]
        stt("dve", ob.ap, OCb[1][:, 0:w].ap, neglam[:, l:l + 1].ap, OCb[0][:, 0:w].ap, ALU.mult, ALU.add,
            [OCb[1][:, 0:w], OCb[0][:, 0:w], neglam[:]], [ob])
        sq = SQb[:, 0:w]
        tt("dve", sq.ap, ob.ap, ob.ap, ALU.mult, [ob], [sq])
        old_mode = rot.get("mode", "all")
        rot["mode"] = "sample"
        b = bank()
        rot["mode"] = old_mode
        pe_mm([(PS[b][:, 0:w], onesdiv[:].ap, sq.ap, True, True)], [onesdiv[:], sq], [psk(b)])
        rs = RSb[:, 0:w]
        act(rs.ap, PS[b][:, 0:w], AF.Ln, [], [psk(b), rs], bias=RMS_EPS)
        act(rs.ap, rs.ap, AF.Exp, [rs], [rs], scale=-0.5)
        dst = obT[:, h, c0:c0 + w]
        stt("dve", dst.ap, ob.ap, gsc[:, 4 * l + h:4 * l + h + 1].ap, rs.ap, ALU.mult, ALU.mult, [ob, rs, gsc[:]], [dst])

    def phase_C(l, q4, wgv, wbv):
        for g, (c0, w) in enumerate(groups):
            for k in range(2):
                nchk = 2 * q4 + k
                res = []
                for bi, src in enumerate((oaT, obT)):
                    b1 = bank()
                    pe_mm([(PS[b1][:, 0:w], wgv[:, c, bi * 256 + k * 128:bi * 256 + (k + 1) * 128].ap, xT[:, c, c0:c0 + w].ap, c == 0, c == NCH - 1) for c in range(NCH)],
                          [xT[:, :, c0:c0 + w], wgv[:]], [psk(b1)])
                    sg = CS[bi][:, 0:w]
                    act(sg.ap, PS[b1][:, 0:w], AF.Sigmoid, [], [psk(b1), sg])
                    b2 = bank()
                    pe_mm([(PS[b2][:, 0:w], wbv[:, bi * 4 + c, k * 128:(k + 1) * 128].ap, src[:, c, c0:c0 + w].ap, c == 0, c == 3) for c in range(4)],
                          [src[:, :, c0:c0 + w], wbv[:]], [psk(b2)])
                    t_ = CS[2 + bi][:, 0:w]
                    tt("dve", t_.ap, PS[b2][:, 0:w], sg.ap, ALU.mult, [sg], [psk(b2), t_])
                    res.append(t_)
                dst = mergedT[:, nchk, c0:c0 + w]
                tt("pool", dst.ap, res[0].ap, res[1].ap, ALU.add, res, [dst])

    def phase_D(l, wo0, wo1):
        dma("sp", LNG[:].ap, ln1_g[l:l + 1, :].to_broadcast([128, D]), (), [LNG[:]])
        dma("sp", LNB[:].ap, ln1_b[l:l + 1, :].to_broadcast([128, D]), (), [LNB[:]])
        pend = {}
        for it in range(NT + 3):
            if it - 3 >= 0 and it - 3 < NT:
                ln_stage2(it - 3, LNG, LNB, l, False, True)
            if it < NT:
                i = it
                r0, n = tiles[i]
                xr = xrow[i % 2][0:n, :]
                if l == 0:
                    dma("sp", xr.ap, x0[r0:r0 + n, :], (), [xr])
                else:
                    dma("sp", xr.ap, xres[r0:r0 + n, :], [("xres", i)], [xr])
                bs = []
                for nh, wv_ in enumerate((wo0, wo1)):
                    b = bank()
                    pe_mm([(PS[b][0:n, :], mergedT[:, c, r0:r0 + n].ap, wv_[:, c, :].ap, c == 0, c == NCH - 1) for c in range(NCH)],
                          [mergedT[:, :, r0:r0 + n], wv_[:]], [psk(b)])
                    bs.append(b)
                pend[i] = bs
            if it - 1 >= 0 and it - 1 < NT:
                i = it - 1
                r0, n = tiles[i]
                xr = xrow[i % 2][0:n, :]
                for nh in range(2):
                    b = pend[i][nh]
                    zz = zacc[0:n, i, nh * 512:(nh + 1) * 512]
                    stt("dve", zz.ap, xrow[i % 2][0:n, nh * 512:(nh + 1) * 512].ap, ALPHA, PS[b][0:n, :], ALU.mult, ALU.add, [xr], [psk(b), zz])
                ln_stage1(i)

    def phase_E1(l, nh, wpg, wpp):
        def load_p(i):
            r0, n = tiles[i]
            pk = ptok[i % 2][0:n, :]
            dma("pool", pk.ap, p_in[l, r0:r0 + n, :], (), [pk])
        load_p(0)
        for i, (r0, n) in enumerate(tiles):
            if i + 1 < NT:
                load_p(i + 1)
            ptt = pTt[i % 2]
            pk = ptok[i % 2][0:n, :]
            b = bank()
            pe_tr([(PSB[b][:, c * 128:c * 128 + n], pk.ap[0:n, c * 128:(c + 1) * 128], ident[0:n, 0:n].ap) for c in range(2)], [pk, ident[:]], [psk(b)])
            src = PSB[b][:, 0:256].rearrange("p (c t) -> p c t", t=128)[:, :, 0:n]
            copy("dve", ptt[:, :, 0:n].ap, src, [], [psk(b), ptt[:, :, 0:n]])
            b1 = bank()
            pe_mm([(PS[b1][0:n, :], xT[:, c, r0:r0 + n].ap, wpg[:, c, :].ap, c == 0, c == NCH - 1) for c in range(NCH)],
                  [xT[:, :, r0:r0 + n], wpg[:]], [psk(b1)])
            sg = SGe[i % 2][0:n, :]
            act(sg.ap, PS[b1][0:n, :], AF.Sigmoid, [], [psk(b1), sg])
            b2 = bank()
            pe_mm([(PS[b2][0:n, :], ptt[:, c, 0:n].ap, wpp[:, c, :].ap, c == 0, c == 1) for c in range(2)], [ptt[:, :, 0:n], wpp[:]], [psk(b2)])
            pl = PLe[0][0:n, :]
            tt("dve", pl.ap, PS[b2][0:n, :], sg.ap, ALU.mult, [sg], [psk(b2), pl])
            zz = zacc[0:n, i, nh * 512:(nh + 1) * 512]
            stt("dve", zz.ap, zz.ap, ALPHA, pl.ap, ALU.mult, ALU.add, [zz, pl], [zz])

    def phase_E2gu(l, fw, wgv, wuv):
        nfc = fw // 128
        for fc in range(nfc):
            for g, (c0, w) in enumerate(groups):
                b1 = bank()
                pe_mm([(PS[b1][:, 0:w], wgv[:, c, fc * 128:(fc + 1) * 128].ap, xT[:, c, c0:c0 + w].ap, c == 0, c == NCH - 1) for c in range(NCH)],
                      [xT[:, :, c0:c0 + w], wgv[:]], [psk(b1)])
                sl = SLe[(fc * NG + g) % 2][:, 0:w]
                act(sl.ap, PS[b1][:, 0:w], AF.Silu, [], [psk(b1), sl])
                b2 = bank()
                pe_mm([(PS[b2][:, 0:w], wuv[:, c, fc * 128:(fc + 1) * 128].ap, xT[:, c, c0:c0 + w].ap, c == 0, c == NCH - 1) for c in range(NCH)],
                      [xT[:, :, c0:c0 + w], wuv[:]], [psk(b2)])
                dst = hT[:, fc, c0:c0 + w]
                tt("dve", dst.ap, PS[b2][:, 0:w], sl.ap, ALU.mult, [sl], [psk(b2), dst])

    def phase_E2d(l, fw, wdv, last):
        nfc = fw // 128
        if last:
            dma("sp", LNG[:].ap, ln2_g[l:l + 1, :].to_broadcast([128, D]), (), [LNG[:]])
            dma("sp", LNB[:].ap, ln2_b[l:l + 1, :].to_broadcast([128, D]), (), [LNB[:]])
        fo, mk = l == DEPTH - 1, l < DEPTH - 1
        pend = {}
        for it in range(NT + 3):
            if last and 0 <= it - 3 < NT:
                ln_stage2(it - 3, LNG, LNB, l, fo, mk)
            if it < NT:
                i = it
                r0, n = tiles[i]
                bs = []
                for nh in range(2):
                    b = bank()
                    pe_mm([(PS[b][0:n, :], hT[:, fc, r0:r0 + n].ap, wdv[:, fc, nh * 512:(nh + 1) * 512].ap, fc == 0, fc == nfc - 1) for fc in range(nfc)],
                          [hT[:, 0:nfc, r0:r0 + n], wdv[:]], [psk(b)])
                    bs.append(b)
                pend[i] = bs
            if 0 <= it - 1 < NT:
                i = it - 1
                r0, n = tiles[i]
                for nh in range(2):
                    b = pend[i][nh]
                    zz = zacc[0:n, i, nh * 512:(nh + 1) * 512]
                    tt("dve", zz.ap, PS[b][0:n, :], zz.ap, ALU.add, [zz], [psk(b), zz])
                if last:
                    ln_stage1(i)

    steps = []

    def add_step(loads, body):
        steps.append((loads, body))

    for l in range(DEPTH):
        W = w_in[l]
        add_step([("dk", W, 8, 2056, 512), ("dv", W, 8, 2568, 512)],
                 lambda v, l=l: (rq_consts_only(), phase_A(l, 1, v["dk"], v["dv"])))
        add_step([("q", W, 8, 1544, 512), ("k", W, 8, 2056, 512)],
                 lambda v, l=l: phase_B(l, 1, v["q"], v["k"]))
        add_step([("fk", W, 8, 512, 512), ("fv", W, 8, 1024, 512)],
                 lambda v, l=l: (f_chain(l), phase_A(l, 0, v["fk"], v["fv"]), f_chain_b(l)))
        add_step([("q", W, 8, 0, 512), ("k", W, 8, 512, 512)],
                 lambda v, l=l: phase_B(l, 0, v["q"], v["k"]))
        for q4 in range(4):
            add_step([("g", None, q4), ("b", None, q4)], lambda v, l=l, q4=q4: phase_C(l, q4, v["g"], v["b"]))
        add_step([("o0", w_o[l], 8, 0, 512), ("o1", w_o[l], 8, 512, 512)], lambda v, l=l: phase_D(l, v["o0"], v["o1"]))
        for nh in range(2):
            add_step([("pg", w_pg[l], 8, nh * 512, 512), ("pp", w_pp[l], 2, nh * 512, 512)],
                     lambda v, l=l, nh=nh: phase_E1(l, nh, v["pg"], v["pp"]))
        f0 = 0
        fparts = []
        while f0 < DFF:
            fw = min(512, DFF - f0)
            fparts.append((f0, fw))
            f0 += fw
        for k, (f0, fw) in enumerate(fparts):
            add_step([("wg", w_g[l], 8, f0, fw), ("wu", w_u[l], 8, f0, fw)],
                     lambda v, l=l, fw=fw: phase_E2gu(l, fw, v["wg"], v["wu"]))
            add_step([("wd", w_d[l][f0:f0 + fw, :], fw // 128, 0, D)],
                     lambda v, l=l, fw=fw, last=(k == len(fparts) - 1): phase_E2d(l, fw, v["wd"], last))

    def do_loads(loads, l):
        views = {}
        for spec in loads:
            nm = spec[0]
            if nm == "g":
                q4 = spec[2]
                v = new_slot((8, 512))
                wload(w_in[l], 8, 3080 + q4 * 256, 256, v, 0, 0)
                wload(w_in[l], 8, 4104 + q4 * 256, 256, v, 0, 256)
            elif nm == "b":
                q4 = spec[2]
                v = new_slot((8, 256))
                wload(w_ba[l], 4, q4 * 256, 256, v, 0, 0)
                wload(w_bb[l], 4, q4 * 256, 256, v, 4, 0)
            else:
                _, src, nch, col0, ncols = spec
                v = new_slot((nch, ncols))
                wload(src, nch, col0, ncols, v, 0, 0)
            views[nm] = v
        return views

    nsteps_per_layer = len(steps) // DEPTH
    pending = do_loads(steps[0][0], 0)
    for k, (loads, body) in enumerate(steps):
        cur = pending
        if k + 1 < len(steps):
            nl = len(steps[k + 1][0])
            cur_slots = {v.buf for v in cur.values()}
            tries = 0
            while tries < 4 and any(slots[(slot_rr[0] + t) % 4] in cur_slots for t in range(nl)):
                slot_rr[0] = (slot_rr[0] + 1) % 4
                tries += 1
            pending = do_loads(steps[k + 1][0], (k + 1) // nsteps_per_layer)
        rot["mode"] = "all"
        body(cur)

    fin = P.op("sp", None, (), ())
    P.ops[fin].deps = set(out_dmas)

    with ExitStack() as stack:
        streams = P.emit(nc, stack)
        with nc.Block() as block:
            @block.tensor
            def _(e):
                replay(e, streams["pe"])

            @block.scalar
            def _(e):
                replay(e, streams["act"])

            @block.vector
            def _(e):
                replay(e, streams["dve"])

            @block.gpsimd
            def _(e):
                replay(e, streams["pool"])

            @block.sync
            def _(e):
                replay(e, streams["sp"])
    return nc, P.stats


WNAMES = ["w_in", "b_forget", "diff_norm_g", "w_branch_fox", "w_branch_diff", "w_out", "ln1_g", "ln1_b",
          "w_ffn_gate", "w_ffn_up", "w_ffn_down", "w_ple_gate", "w_ple_proj", "ln2_g", "ln2_b"]


def run(cfg, inputs, ncores, trace=False):
    TP, PAST, DEPTH = cfg["TP"], cfg["PAST"], cfg["DEPTH"]
    T = TP + TS
    nc, stats = build(cfg)
    consts = make_consts(TP, PAST)
    f = lambda a: np.ascontiguousarray(np.asarray(a, dtype=np.float32))
    shared = {k: f(inputs[k]) for k in WNAMES}
    for k in ("lambda_q1", "lambda_k1", "lambda_q2", "lambda_k2"):
        shared[k] = f(inputs[k]).reshape(1, DEPTH * 64)
    shared.update(consts)
    xp, xs = f(inputs["x_prompt"]), f(inputs["x_sample"])
    pp, ps_ = f(inputs["p_prompt"]), f(inputs["p_sample"])
    in_maps = []
    for b in range(ncores):
        m = dict(shared)
        m["x0"] = np.concatenate([xp[b], xs[b]], axis=0)
        m["p"] = np.concatenate([pp[:, b], ps_[:, b]], axis=1)
        m["ckf"] = f(inputs["cache_fox_k"][:, b]).reshape(DEPTH, PAST, 512)
        m["cvf"] = f(inputs["cache_fox_v"][:, b]).reshape(DEPTH, PAST, 512)
        m["clf"] = f(inputs["cache_fox_logf"][:, b]).reshape(DEPTH, PAST, 8)
        m["ckd"] = f(inputs["cache_diff_k"][:, b]).reshape(DEPTH, PAST, 512)
        m["cvd"] = f(inputs["cache_diff_v"][:, b]).reshape(DEPTH, PAST, 512)
        in_maps.append(m)
    res = run_bass_kernel_spmd(nc, in_maps, core_ids=list(range(ncores)), trace=trace)
    rs = res.results
    B = ncores
    st = lambda k: np.stack([np.asarray(r[k]) for r in rs], axis=0)
    y = st("y")
    fk, fv, lf, dk, dv = st("o_fk"), st("o_fv"), st("o_lf"), st("o_dk"), st("o_dv")
    tr = lambda a: np.ascontiguousarray(np.moveaxis(a, 0, 1))
    fk, fv, lf, dk, dv = tr(fk), tr(fv), tr(lf), tr(dk), tr(dv)
    outs = (
        y[:, :TP], y[:, TP:],
        fk[:, :, :TP].reshape(DEPTH, B, TP, 8, 64), fv[:, :, :TP].reshape(DEPTH, B, TP, 8, 64), lf[:, :, :TP],
        dk[:, :, :TP].reshape(DEPTH, B, TP, 4, 2, 64), dv[:, :, :TP].reshape(DEPTH, B, TP, 4, 128),
        fk[:, :, TP:].reshape(DEPTH, B, TS, 8, 64), fv[:, :, TP:].reshape(DEPTH, B, TS, 8, 64), lf[:, :, TP:],
        dk[:, :, TP:].reshape(DEPTH, B, TS, 4, 2, 64), dv[:, :, TP:].reshape(DEPTH, B, TS, 4, 128),
    )
    outs = tuple(np.ascontiguousarray(o, dtype=np.float32) for o in outs)
    dbgs = {k[4:]: st(k) for k in rs[0].keys() if k.startswith("dbg_")}
    return outs, res, stats, dbgs


def kernel(**inputs):
    cfg = dict(TP=2048, PAST=4096, DEPTH=4)
    outs, _, _, _ = run(cfg, inputs, 8)
    return outs
```
